# Optimizing a Trainium2 kernel written in Bass

```python
import jax, jax.numpy as jnp
from jax import lax
import numpy as np

D_MODEL = 2048
BATCH = 4
SEQ = 2048
DEPTH = 1
DEC_BATCH = 32
DEC_SEQ = 4
PAST_LEN = 8192
PAGE_SIZE = 128

HEAD_DIM = 128
HEADS_PER_GROUP = 4
DIL_GROUPS = ((128, 1), (512, 4), (2048, 16))
N_GROUPS = len(DIL_GROUPS)
N_HEADS = N_GROUPS * HEADS_PER_GROUP
ATT_WIDTH = N_HEADS * HEAD_DIM
ATT_OUT = HEADS_PER_GROUP * HEAD_DIM
SPAN = 128
ATT_SCALE = HEAD_DIM ** -0.5
ROT_DIM = HEAD_DIM // 4
ROPE_THETA = 500000.0
D_CONV = D_MODEL // 2
CONV_WIDTH = 31
FFN_HIDDEN = ((8 * D_MODEL + 3 * 256 - 1) // (3 * 256)) * 256
NORM_EPS = 1e-6
LN_EPS = 1e-5
IN_COLS = 2 * D_CONV + 3 * ATT_WIDTH + 2 * D_MODEL

kernel_name = "gated_conformer_dilated_swa_decoder_step"


def rms_norm(x, g):
    xf = x.astype(jnp.float32)
    y = xf * lax.rsqrt(jnp.mean(xf * xf, axis=-1, keepdims=True) + NORM_EPS)
    return (y * g.astype(jnp.float32)).astype(x.dtype)


def layer_norm(x, g, b):
    xf = x.astype(jnp.float32)
    mu = jnp.mean(xf, axis=-1, keepdims=True)
    xc = xf - mu
    var = jnp.mean(xc * xc, axis=-1, keepdims=True)
    y = xc * lax.rsqrt(var + LN_EPS) * g.astype(jnp.float32) + b.astype(jnp.float32)
    return y.astype(x.dtype)


def rope(x, pos):
    half = ROT_DIM // 2
    inv = jnp.power(ROPE_THETA, -jnp.arange(0, ROT_DIM, 2, dtype=jnp.float32) / ROT_DIM)
    ang = pos.astype(jnp.float32)[:, None] * inv[None, :]
    cos = jnp.cos(ang)[None, :, None, :]
    sin = jnp.sin(ang)[None, :, None, :]
    xf = x.astype(jnp.float32)
    x1 = xf[..., :half]
    x2 = xf[..., half:ROT_DIM]
    out = jnp.concatenate([x1 * cos - x2 * sin, x2 * cos + x1 * sin, xf[..., ROT_DIM:]], axis=-1)
    return out.astype(x.dtype)


def band_attention(q, k, v):
    N, L, H, hd = q.shape
    nb = -(-L // SPAN)
    Lp = nb * SPAN
    pad = ((0, 0), (0, Lp - L), (0, 0), (0, 0))
    qb = jnp.pad(q, pad).reshape(N, nb, SPAN, H, hd)
    kb = jnp.pad(k, pad).reshape(N, nb, SPAN, H, hd)
    vb = jnp.pad(v, pad).reshape(N, nb, SPAN, H, hd)

    def with_prev(t):
        prev = jnp.pad(t, ((0, 0), (1, 0), (0, 0), (0, 0), (0, 0)))[:, :-1]
        return jnp.concatenate([prev, t], axis=2)

    kk = with_prev(kb)
    vv = with_prev(vb)
    s = jnp.einsum('nbqhd,nbkhd->nbhqk', qb, kk,
                   preferred_element_type=jnp.float32) * ATT_SCALE
    qi = jnp.arange(SPAN)[:, None]
    ki = jnp.arange(2 * SPAN)[None, :]
    dist = SPAN + qi - ki
    blk = jnp.arange(nb)[:, None, None]
    valid = ((dist >= 0) & (dist <= SPAN))[None] & ((blk * SPAN - SPAN + ki[None]) >= 0)
    s = jnp.where(valid[None, :, None], s, -jnp.inf)
    m = jnp.max(s, axis=-1, keepdims=True)
    p = jnp.exp(s - m)
    den = jnp.sum(p, axis=-1)
    o = jnp.einsum('nbhqk,nbkhd->nbqhd', p, vv.astype(jnp.float32))
    o = o / jnp.transpose(den, (0, 1, 3, 2))[..., None]
    lse = jnp.transpose(m[..., 0] + jnp.log(den), (0, 1, 3, 2))
    o = o.reshape(N, Lp, H, hd)[:, :L].astype(q.dtype)
    lse = lse.reshape(N, Lp, H)[:, :L]
    return o, lse


def dilated_prompt(q, k, v, dil):
    B, S, H, hd = q.shape

    def to_streams(t):
        return t.reshape(B, S // dil, dil, H, hd).transpose(0, 2, 1, 3, 4).reshape(B * dil, S // dil, H, hd)

    o, lse = band_attention(to_streams(q), to_streams(k), to_streams(v))
    o = o.reshape(B, dil, S // dil, H, hd).transpose(0, 2, 1, 3, 4).reshape(B, S, H, hd)
    lse = lse.reshape(B, dil, S // dil, H).transpose(0, 2, 1, 3).reshape(B, S, H)
    return o, lse


def dilated_sample(q, k_new, v_new, k_buf, v_buf, dil):
    Wb = k_buf.shape[1]
    T = q.shape[1]
    kc = jnp.concatenate([k_buf, k_new], axis=1)
    vc = jnp.concatenate([v_buf, v_new], axis=1)
    i = jnp.arange(T)[:, None]
    j = jnp.arange(SPAN + 1)[None, :]
    idx = Wb + i - j * dil
    valid = idx >= 0
    idx = jnp.maximum(idx, 0)
    kg = kc[:, idx]
    vg = vc[:, idx]
    s = jnp.einsum('bthd,btjhd->bthj', q, kg,
                   preferred_element_type=jnp.float32) * ATT_SCALE
    s = jnp.where(valid[None, :, None, :], s, -jnp.inf)
    m = jnp.max(s, axis=-1, keepdims=True)
    p = jnp.exp(s - m)
    den = jnp.sum(p, axis=-1)
    o = jnp.einsum('bthj,btjhd->bthd', p, vg.astype(jnp.float32)) / den[..., None]
    lse = m[..., 0] + jnp.log(den)
    return o.astype(q.dtype), lse, kc[:, T:], vc[:, T:]


def combine_groups(outs, lses):
    w = jax.nn.softmax(jnp.stack(lses, axis=0), axis=0)
    o = jnp.einsum('gnlh,gnlhd->nlhd', w, jnp.stack(outs, axis=0).astype(jnp.float32))
    return o.astype(outs[0].dtype)


def head_group(t, g):
    return t[:, :, g * HEADS_PER_GROUP:(g + 1) * HEADS_PER_GROUP]


def prompt_attention(q, k, v):
    S = q.shape[1]
    outs, lses, new_k, new_v = [], [], [], []
    for g, (win, dil) in enumerate(DIL_GROUPS):
        kg, vg = head_group(k, g), head_group(v, g)
        o, lse = dilated_prompt(head_group(q, g), kg, vg, dil)
        outs.append(o)
        lses.append(lse)
        keep = min(win, S)
        new_k.append(kg[:, S - keep:])
        new_v.append(vg[:, S - keep:])
    return combine_groups(outs, lses), new_k, new_v


def sample_attention(q, k, v, k_bufs, v_bufs):
    outs, lses, new_k, new_v = [], [], [], []
    for g, (win, dil) in enumerate(DIL_GROUPS):
        o, lse, nk, nv = dilated_sample(head_group(q, g), head_group(k, g), head_group(v, g),
                                        k_bufs[g], v_bufs[g], dil)
        outs.append(o)
        lses.append(lse)
        new_k.append(nk)
        new_v.append(nv)
    return combine_groups(outs, lses), new_k, new_v


def conv_branch(u, u_past, w_dw, b_dw, ln_g, ln_b, w_pw, b_pw):
    uc = jnp.concatenate([u_past, u], axis=1)
    y = lax.conv_general_dilated(uc, w_dw[:, None, :], window_strides=(1,), padding='VALID',
                                 dimension_numbers=('NWC', 'WIO', 'NWC'),
                                 feature_group_count=D_CONV) + b_dw
    y = jax.nn.silu(layer_norm(y, ln_g, ln_b))
    return y @ w_pw + b_pw, uc[:, -(CONV_WIDTH - 1):]


def decoder_layer(x, pos, u_past, attend, g_mix, w_in, b_glu, w_dw, b_dw, ln_g, ln_b,
                  w_pw, b_pw, w_o_att, w_out, g_ffn, w_gate, w_up, w_down):
    N, L, _ = x.shape
    h = rms_norm(x, g_mix)
    z = h @ w_in
    o1 = 2 * D_CONV
    o2 = o1 + ATT_WIDTH
    o3 = o2 + ATT_WIDTH
    o4 = o3 + ATT_WIDTH
    o5 = o4 + D_MODEL
    glu = z[..., :o1] + b_glu
    u = glu[..., :D_CONV] * jax.nn.sigmoid(glu[..., D_CONV:])
    q = rope(z[..., o1:o2].reshape(N, L, N_HEADS, HEAD_DIM), pos)
    k = rope(z[..., o2:o3].reshape(N, L, N_HEADS, HEAD_DIM), pos)
    v = z[..., o3:o4].reshape(N, L, N_HEADS, HEAD_DIM)
    gate_conv = jax.nn.sigmoid(z[..., o4:o5])
    gate_attn = jax.nn.sigmoid(z[..., o5:])

    conv_out, new_conv = conv_branch(u, u_past, w_dw, b_dw, ln_g, ln_b, w_pw, b_pw)
    o_att, new_k, new_v = attend(q, k, v)
    attn_out = o_att.reshape(N, L, ATT_OUT) @ w_o_att

    x = x + (gate_conv * conv_out + gate_attn * attn_out) @ w_out
    hf = rms_norm(x, g_ffn)
    x = x + (jax.nn.silu(hf @ w_gate) * (hf @ w_up)) @ w_down
    return x, new_conv, new_k, new_v


def setup_inputs(seed: int = 0) -> dict:
    key = jax.random.key(seed)
    ks = iter(jax.random.split(key, 40))
    f32 = jnp.float32

    def nrm(shape, scale):
        return jax.random.normal(next(ks), shape, f32) * scale

    inp = {}
    inp['x_prompt'] = nrm((BATCH, SEQ, D_MODEL), 1.0)
    inp['x_sample'] = nrm((DEC_BATCH, DEC_SEQ, D_MODEL), 1.0)
    inp['state_conv'] = nrm((DEPTH, DEC_BATCH, CONV_WIDTH - 1, D_CONV), 1.0)
    for win, _ in DIL_GROUPS:
        wb = min(win, PAST_LEN)
        inp['cache_k_w%d' % win] = nrm((DEPTH, DEC_BATCH, wb, HEADS_PER_GROUP, HEAD_DIM), 1.0)
        inp['cache_v_w%d' % win] = nrm((DEPTH, DEC_BATCH, wb, HEADS_PER_GROUP, HEAD_DIM), 1.0)
    inp['g_mix'] = 1.0 + nrm((DEPTH, D_MODEL), 0.02)
    inp['w_in'] = nrm((DEPTH, D_MODEL, IN_COLS), D_MODEL ** -0.5)
    inp['b_glu'] = nrm((DEPTH, 2 * D_CONV), 0.02)
    inp['w_dw'] = nrm((DEPTH, CONV_WIDTH, D_CONV), CONV_WIDTH ** -0.5)
    inp['b_dw'] = nrm((DEPTH, D_CONV), 0.02)
    inp['ln_g'] = 1.0 + nrm((DEPTH, D_CONV), 0.02)
    inp['ln_b'] = nrm((DEPTH, D_CONV), 0.02)
    inp['w_pw'] = nrm((DEPTH, D_CONV, D_MODEL), D_CONV ** -0.5)
    inp['b_pw'] = nrm((DEPTH, D_MODEL), 0.02)
    inp['w_o_att'] = nrm((DEPTH, ATT_OUT, D_MODEL), ATT_OUT ** -0.5)
    inp['w_out'] = nrm((DEPTH, D_MODEL, D_MODEL), D_MODEL ** -0.5)
    inp['g_ffn'] = 1.0 + nrm((DEPTH, D_MODEL), 0.02)
    inp['w_gate'] = nrm((DEPTH, D_MODEL, FFN_HIDDEN), D_MODEL ** -0.5)
    inp['w_up'] = nrm((DEPTH, D_MODEL, FFN_HIDDEN), D_MODEL ** -0.5)
    inp['w_down'] = nrm((DEPTH, FFN_HIDDEN, D_MODEL), FFN_HIDDEN ** -0.5)
    inp['g_final'] = 1.0 + nrm((D_MODEL,), 0.02)
    return inp


def reference(x_prompt, x_sample, state_conv, cache_k_w128, cache_v_w128, cache_k_w512,
              cache_v_w512, cache_k_w2048, cache_v_w2048, g_mix, w_in, b_glu, w_dw, b_dw,
              ln_g, ln_b, w_pw, b_pw, w_o_att, w_out, g_ffn, w_gate, w_up, w_down, g_final):
    pos_p = jnp.arange(x_prompt.shape[1], dtype=jnp.int32)
    pos_s = PAST_LEN + jnp.arange(x_sample.shape[1], dtype=jnp.int32)
    xp, xs = x_prompt, x_sample
    conv_p, conv_s = [], []
    kp = [[] for _ in DIL_GROUPS]
    vp = [[] for _ in DIL_GROUPS]
    ks_ = [[] for _ in DIL_GROUPS]
    vs_ = [[] for _ in DIL_GROUPS]
    for l in range(DEPTH):
        lw = (g_mix[l], w_in[l], b_glu[l], w_dw[l], b_dw[l], ln_g[l], ln_b[l], w_pw[l], b_pw[l],
              w_o_att[l], w_out[l], g_ffn[l], w_gate[l], w_up[l], w_down[l])
        zero_past = jnp.zeros((xp.shape[0], CONV_WIDTH - 1, D_CONV), xp.dtype)
        xp, cp, nkp, nvp = decoder_layer(xp, pos_p, zero_past, prompt_attention, *lw)
        k_bufs = (cache_k_w128[l], cache_k_w512[l], cache_k_w2048[l])
        v_bufs = (cache_v_w128[l], cache_v_w512[l], cache_v_w2048[l])
        attend_s = lambda q, k, v, kb=k_bufs, vb=v_bufs: sample_attention(q, k, v, kb, vb)
        xs, cs, nks, nvs = decoder_layer(xs, pos_s, state_conv[l], attend_s, *lw)
        conv_p.append(cp)
        conv_s.append(cs)
        for g in range(N_GROUPS):
            kp[g].append(nkp[g])
            vp[g].append(nvp[g])
            ks_[g].append(nks[g])
            vs_[g].append(nvs[g])
    y_prompt = rms_norm(xp, g_final)
    y_sample = rms_norm(xs, g_final)
    st = lambda lst: jnp.stack(lst, axis=0)
    return (y_prompt, y_sample, st(conv_p), st(conv_s),
            st(kp[0]), st(ks_[0]), st(vp[0]), st(vs_[0]),
            st(kp[1]), st(ks_[1]), st(vp[1]), st(vs_[1]),
            st(kp[2]), st(ks_[2]), st(vp[2]), st(vs_[2]))
```

```python
import contextlib
import numpy as np
import concourse.bass as bass
import concourse.mybir as mybir
from concourse.bass_utils import run_bass_kernel_spmd

F32 = mybir.dt.float32
BF16 = mybir.dt.bfloat16
AF = mybir.ActivationFunctionType
ALU = mybir.AluOpType
AX = mybir.AxisListType

D = 2048
KC = 16
HID = 5632
NHS = 11
O1, O2, O3, O4, O5 = 2048, 3584, 5120, 6656, 8704
GROUPS = ((128, 1), (512, 4), (2048, 16))
SCALE = 128 ** -0.5
NEG = -30000.0
NORM_EPS = 1e-6
LN_EPS = 1e-5
DEBUG_MAXBLK = [10 ** 9]
DEBUG_CUT = [99]
A2_PROD_ENG = ['dve']
CAST_ENG = ['dve']
DEBUG_NOROPE = [0]
NCOL = 1152
PAST = 8192


class _Rec:
    def __init__(self):
        self.call = None

    def __getattr__(self, meth):
        def f(*a, **kw):
            self.call = (meth, a, kw)
            return self
        return f


class Sched:
    COMPUTE = ('pe', 'act', 'dve', 'pool')

    def __init__(self, nc, stack, ring=8):
        self.nc = nc
        self.q = {e: [] for e in ('pe', 'act', 'dve', 'pool', 'sp')}
        self.sems = {}
        self.cnt = {}
        for e in self.COMPUTE:
            self.sems['p_' + e] = stack.enter_context(nc.semaphore('p_' + e))
            self.cnt['p_' + e] = 0
        self.rings = {}
        for qn in ('sp', 'act', 'pool'):
            names = []
            for i in range(ring):
                nm = 'd_%s%d' % (qn, i)
                self.sems[nm] = stack.enter_context(nc.semaphore(nm))
                self.cnt[nm] = 0
                names.append(nm)
            self.rings[qn] = [names, 0]
        self.last_w = {}
        self.readers = {}
        self.waited = {e: {} for e in self.q}
        self.nwaits = 0
        self.ninst = {e: 0 for e in self.q}

    def _deps(self, reads, writes):
        deps = []
        for k in reads:
            t = self.last_w.get(k)
            if t is not None:
                deps.append(t)
        for k in writes:
            t = self.last_w.get(k)
            if t is not None:
                deps.append(t)
            rd = self.readers.get(k)
            if rd:
                deps.extend(rd.items())
        return deps

    def _wait(self, eng, tok, force=False):
        sem, val = tok
        if not force and eng == 'pe' and sem == 'p_pe':
            return
        if self.waited[eng].get(sem, 0) >= val:
            return
        self.waited[eng][sem] = val
        s = self.sems[sem]
        self.q[eng].append(lambda e, s=s, val=val: e.wait_ge(s, val))
        self.nwaits += 1

    def _record(self, tok, reads, writes):
        sem, val = tok
        for k in reads:
            d = self.readers.setdefault(k, {})
            if d.get(sem, 0) < val:
                d[sem] = val
        for k in writes:
            self.last_w[k] = tok
            self.readers[k] = {}

    def op(self, eng, fn, reads=(), writes=()):
        pk = [k for k in reads if isinstance(k, tuple) and k[0] == 'ps']
        if pk:
            reads = [k for k in reads if not (isinstance(k, tuple) and k[0] == 'ps')]
            writes = list(writes) + pk
        for t in self._deps(reads, writes):
            self._wait(eng, t)
        nm = 'p_' + eng
        self.cnt[nm] += 1
        s = self.sems[nm]
        rec = _Rec()
        fn(rec)
        meth, a, kw = rec.call
        self.q[eng].append(lambda e, meth=meth, a=a, kw=kw, s=s: getattr(e, meth)(*a, **kw).then_inc(s, 1))
        tok = (nm, self.cnt[nm])
        self._record(tok, reads, writes)
        self.ninst[eng] += 1
        return tok

    def dma(self, queue, out, in_, reads=(), writes=(), **kw):
        names, idx = self.rings[queue]
        nm = names[idx % len(names)]
        self.rings[queue][1] = idx + 1
        if self.cnt[nm] > 0:
            self._wait(queue, (nm, self.cnt[nm]), force=True)
        for t in self._deps(reads, writes):
            self._wait(queue, t, force=True)
        self.cnt[nm] += 16
        s = self.sems[nm]
        self.q[queue].append(
            lambda e, s=s, out=out, in_=in_, kw=kw: e.dma_start(out=out, in_=in_, **kw).then_inc(s, 16))
        tok = (nm, self.cnt[nm])
        self._record(tok, reads, writes)
        self.ninst[queue] += 1
        return tok

    def barrier(self, engines=('pe', 'act', 'dve', 'pool', 'sp')):
        toks = [(nm, c) for nm, c in self.cnt.items() if c > 0 and not nm.startswith('d_pool')]
        for e in engines:
            for t in toks:
                self._wait(e, t, force=True)

    def emit(self):
        nc = self.nc
        with nc.Block() as block:
            @block.tensor
            def _(e):
                for f in self.q['pe']:
                    f(e)

            @block.scalar
            def _(e):
                for f in self.q['act']:
                    f(e)

            @block.vector
            def _(e):
                for f in self.q['dve']:
                    f(e)

            @block.gpsimd
            def _(e):
                for f in self.q['pool']:
                    f(e)

            @block.sync
            def _(e):
                for f in self.q['sp']:
                    f(e)


def v3(ap, a):
    return ap.rearrange("p (a b) -> p a b", a=a)


def mkap(ap, dims):
    return bass.AP(ap.tensor, ap.offset, [list(ap.ap[0])] + [list(d) for d in dims])


def group_blocks(g):
    d = GROUPS[g][1]
    if d == 1:
        return [(0, 7, 'kv')] + [(0, 8, 'halo')] + [(0, b, 'full') for b in range(9, 16)]
    if d == 4:
        out = []
        for r in range(4):
            out += [(r, 1, 'kv'), (r, 2, 'halo'), (r, 3, 'full')]
        return out
    return [(r, 0, 'g2') for r in range(16)]


def build_program(stop_after=None, debug=()):
    nc = bass.Bass("TRN2", target_bir_lowering=False)

    def din(name, shape):
        return nc.dram_tensor(name, list(shape), F32, kind="ExternalInput").ap()

    def dout(name, shape, dt=F32):
        return nc.dram_tensor(name, list(shape), dt, kind="ExternalOutput").ap()

    xext = din("xext", [2048, D])
    xs_d = din("xs", [16, D])
    state_d = din("state", [120, 1024])
    ck = [din("ck%d" % g, [4, GROUPS[g][0], 512]) for g in range(3)]
    cv = [din("cv%d" % g, [4, GROUPS[g][0], 512]) for g in range(3)]
    w_in = din("w_in", [D, 10752])
    w_pw = din("w_pw", [1024, D])
    w_o = din("w_o", [512, D])
    w_out = din("w_out", [D, D])
    w_gate = din("w_gate", [D, HID])
    w_up = din("w_up", [D, HID])
    w_down = din("w_down", [HID, D])
    vecs_d = din("vecs", [384, 128])
    gb_d = din("gb", [3, D])
    tabs_d = din("tabs", [3, 128, 16, 64])
    tabS_d = din("tabS", [128, 64])
    masks_d = din("masks", [128, 4, 128])
    ident_d = din("ident", [128, 128])
    smask_d = din("smask", [128, 3, 16])
    sel_d = din("sel", [128, 4, 128])
    ind_d = din("ind", [128, 4])
    hv_d = din("hv", [128, 1])

    y_d = dout("y", [1024, D])
    ys_d = dout("ys", [16, D])
    convp_d = dout("convp", [30, 1024])
    convs_d = dout("convs", [4, 30, 1024])
    own_rows = (128, 512, 1024)
    kp = [dout("kp%d" % g, [own_rows[g], 512]) for g in range(3)]
    vp = [dout("vp%d" % g, [own_rows[g], 512]) for g in range(3)]
    kso = [dout("kso%d" % g, [4, GROUPS[g][0], 512]) for g in range(3)]
    vso = [dout("vso%d" % g, [4, GROUPS[g][0], 512]) for g in range(3)]

    with contextlib.ExitStack() as st:
        S = Sched(nc, st, ring=32)
        ARENA_WORDS = 51456
        arena = st.enter_context(nc.sbuf_tensor("arena", [128, ARENA_WORDS], F32))
        PS = [st.enter_context(nc.psum_tensor("ps%d" % i, [128, 512], F32)) for i in range(8)]
        PSK = [('ps', i) for i in range(8)]

        def psb(i):
            return PS[i][:, :].bitcast(BF16)

        def af(off, n):
            assert off + n <= ARENA_WORDS, (off, n)
            return arena[:, off:off + n]

        def ab(off, nw):
            assert off + nw <= ARENA_WORDS, (off, nw)
            return arena[:, off:off + nw].bitcast(BF16)

        dbg_outs = {}

        def dbg(name, ap, key):
            if name not in debug:
                return
            shape = list(ap.shape)
            dt = ap.dtype
            o = dout("dbg_" + name, shape, dt)
            S.dma('sp', o, ap, reads=(list(key) if isinstance(key, (list, tuple)) and not isinstance(key[0], str) or isinstance(key, list) else [key]), writes=['dbg_' + name])
            dbg_outs[name] = o

        o = 0
        identf = af(o, 128); o += 128
        identb = ab(o, 64); o += 64
        onesf = af(o, 128); o += 128
        onesb = ab(o, 64); o += 64
        mb = v3(ab(o, 256), 4); o += 256
        selb = v3(ab(o, 256), 4); o += 256
        vecT = af(o, 384); o += 384
        ind = af(o, 4); o += 4
        smask = v3(af(o, 48), 3); o += 48
        hv = af(o, 1); o += 4
        tabS = af(o, 64); o += 64
        ssq = af(o, 1); o += 2
        rs = af(o, 1); o += 2
        rstd = af(o, 1); o += 2
        eps_norm = af(o, 1); o += 2
        eps_ln = af(o, 1); o += 2
        o = (o + 63) // 64 * 64
        gbc = af(o, 2048); o += 2048
        stage = af(o, 512); o += 512
        W = []
        for i in range(3):
            W.append(ab(o, 4096)); o += 4096
        OB = o
        WK = ['W0', 'W1', 'W2']

        def mxk(c0, ncols):
            return ['mixT%d' % t for t in range(c0 // 128, min(8, (c0 + ncols - 1) // 128) + 1)]
        MK_ALL = ['mixT%d' % t for t in range(9)]

        def vcol(row):
            return vecT[:, row:row + 1]
        R_BGLU, R_BDW, R_LNG, R_LNB, R_BPW, R_WDW = 16, 32, 40, 48, 56, 128

        S.dma('sp', identf, ident_d, writes=['identf'])
        S.op('dve', lambda e: e.tensor_copy(out=identb, in_=identf), ['identf'], ['identb'])
        S.op('dve', lambda e: e.memset(onesf, 1.0), [], ['onesf'])
        S.op('dve', lambda e: e.memset(onesb, 1.0), [], ['onesb'])
        st4 = v3(stage, 4)
        S.dma('sp', st4, masks_d, writes=['stage'])
        S.op('dve', lambda e: e.tensor_copy(out=mb, in_=st4), ['stage'], ['mb'])
        S.dma('sp', st4, sel_d, reads=[], writes=['stage'])
        S.op('dve', lambda e: e.tensor_copy(out=selb, in_=st4), ['stage'], ['selb'])
        S.dma('sp', ind, ind_d, writes=['ind'])
        S.dma('sp', smask, smask_d, writes=['smask'])
        S.dma('sp', hv, hv_d, writes=['hv'])
        S.dma('sp', tabS, tabS_d, writes=['tabS'])
        vraw = v3(gbc[:, 0:384], 3)
        S.dma('sp', vraw, vecs_d.rearrange("(a p) c -> p a c", p=128), writes=['gbc'])
        for a in range(3):
            S.op('pe', lambda e, a=a: e.transpose(out=PS[0][:, a * 128:(a + 1) * 128], in_=vraw[:, a, :], identity=identf),
                 ['gbc', 'identf'], [PSK[0]])
        S.op('act', lambda e: e.copy(out=vecT, in_=PS[0][:, 0:384]), [PSK[0]], ['vecT'])
        copy_pieces = []
        for g in (2, 1, 0):
            wb = GROUPS[g][0]
            for (dst_, src_, key_) in ((kso[g], ck[g], 'kso%d' % g), (vso[g], cv[g], 'vso%d' % g)):
                if g == 0:
                    copy_pieces.append((dst_[:, 0:wb - 4, :], src_[:, 4:wb, :], key_))
                else:
                    for s_ in range(4):
                        for r0 in range(0, wb - 4, 512):
                            r1 = min(r0 + 512, wb - 4)
                            copy_pieces.append((dst_[s_, r0:r1, :], src_[s_, r0 + 4:r1 + 4, :], key_))

        def issue_copy_piece():
            if copy_pieces:
                d_, s_, k_ = copy_pieces.pop(0)
                S.dma('sp', d_, s_, writes=[k_])
        S.dma('sp', convs_d[:, 0:26, :], state_d.rearrange("(s r) c -> s r c", r=30)[:, 4:30, :], writes=['convs'])

        def load_gbc(row):
            S.dma('sp', gbc, gb_d[row:row + 1, :].partition_broadcast(128), writes=['gbc'])

        def wload(wi, src2d, kc, ncols):
            dst = W[wi][:, 0:kc * ncols].rearrange("p (k n) -> p k n", k=kc)
            S.dma('pool', dst, src2d.rearrange("(k p) n -> p k n", p=128), writes=[WK[wi], WK[wi] + 'h0', WK[wi] + 'h1'])
            return dst

        def norm_tile(src, src_key, hb, hb_key, np_=128):
            src, hb = src[0:np_, :], hb[0:np_, :]
            S.op('act', lambda e: e.activation(out=hb, in_=src, func=AF.Square, accum_out=ssq[0:np_, :]), [src_key], [hb_key, 'ssq'])
            S.op('act', lambda e: e.activation(out=rs[0:np_, :], in_=ssq[0:np_, :], func=AF.Ln, scale=1.0 / D, bias=eps_norm[0:np_, :]), ['ssq', 'epsc'], ['rs'])
            S.op('act', lambda e: e.activation(out=rstd[0:np_, :], in_=rs[0:np_, :], func=AF.Exp, scale=-0.5), ['rs'], ['rstd'])
            S.op('dve', lambda e: e.scalar_tensor_tensor(out=hb, in0=src, scalar=rstd[0:np_, :], in1=gbc[0:np_, :], op0=ALU.mult, op1=ALU.mult),
                 [src_key, 'rstd', 'gbc'], [hb_key])

        tctr = [0]

        def transposes(hb, hb_key, dst_fn, dst_key, banks=(2, 3), np_=128, p0=0):
            for half in range(2):
                bi = banks[half]
                bank = psb(bi)
                for c in range(8):
                    cc = 8 * half + c
                    S.op('pe', lambda e: e.transpose(out=bank[:, c * 128:c * 128 + np_], in_=hb[p0:p0 + np_, cc * 128:(cc + 1) * 128], identity=identb[p0:p0 + np_, p0:p0 + np_]),
                         [hb_key, 'identb'], [PSK[bi]])
                eng = 'act' if (tctr[0] % 2 == 0) else 'dve'
                tctr[0] += 1
                dst = dst_fn(8 * half)
                src_ = v3(bank, 8)[:, :, 0:np_]
                if eng == 'act':
                    S.op('act', lambda e: e.copy(out=dst, in_=src_), [PSK[bi]], [dst_key])
                else:
                    S.op('dve', lambda e: e.tensor_copy(out=dst, in_=src_), [PSK[bi]], [dst_key])

        X_OT = OB + 32784
        oT = v3(ab(X_OT, 2080), 4)

        S.op('dve', lambda e: e.memset(eps_norm, NORM_EPS), [], ['epsc'])
        S.op('dve', lambda e: e.memset(eps_ln, LN_EPS), [], ['epsl'])

        o = OB
        ks_all = v3(af(o, 1536), 3); o += 1536
        vs_all = v3(af(o, 1536), 3); o += 1536
        qs_all = v3(af(o, 1536), 3); o += 1536
        A2_BASE = o
        xsb = af(o, 2048); o += 2048
        xblk = [af(o, 2048), af(o + 2048, 2048)]; o += 4096
        hblk = ab(o, 1024); o += 1024
        hTb = [v3(ab(o, 1024), 16), v3(ab(o + 1024, 1024), 16)]; o += 2048
        kst = [af(o, 512), af(o + 512, 512)]; o += 1024
        vst = [af(o, 512), af(o + 512, 512)]; o += 1024
        qst = af(o, 512); o += 512
        ra = v3(af(o, 128), 4); o += 128
        rb = v3(af(o, 128), 4); o += 128
        kb = ab(o, 256); o += 256
        qb = ab(o, 256); o += 256
        kT = [v3(ab(o, 256), 4), v3(ab(o + 256, 256), 4)]; o += 512
        vb = [ab(o, 256), ab(o + 256, 256)]; o += 512
        qT = v3(ab(o, 256), 4); o += 256
        pT = [ab(o, 256), ab(o + 256, 256)]; o += 512
        num = v3(af(o, 4096), 4); o += 4096
        den = v3(af(o, 4096), 4); o += 4096
        tab = v3(af(o, 1024), 16); o += 1024
        X_TAB2 = o; o += 1024
        assert o <= X_OT, o

        def load_xsb(dst, key):
            S.op('dve', lambda e: e.memset(dst, 0.0), [], [key])
            for s in range(4):
                S.dma('sp', dst[32 * s:32 * s + 4, :], xs_d[4 * s:4 * s + 4, :], writes=[key])

        load_xsb(xsb, 'xsb')
        load_gbc(0)

        blkctr = [0]

        def rope(ps_i, tb, out_st, out_key, tkey='tab'):
            ps3 = v3(PS[ps_i][:, :], 4)
            st3 = v3(out_st, 4)
            cc = mkap(tb[:, 0:32], [[0, 4], [1, 32]])
            ms = mkap(tb[:, 32:48], [[0, 4], [1, 16]])
            pp = mkap(tb[:, 48:64], [[0, 4], [1, 16]])
            S.op('act', lambda e: e.copy(out=out_st, in_=PS[ps_i][:, :]), [PSK[ps_i]], [out_key])
            if DEBUG_NOROPE[0]:
                return
            S.op('dve', lambda e: e.tensor_tensor(out=ra, in0=st3[:, :, 0:32], in1=cc, op=ALU.mult), [out_key, tkey], ['ra'])
            S.op('dve', lambda e: e.tensor_tensor(out=rb[:, :, 0:16], in0=st3[:, :, 16:32], in1=ms, op=ALU.mult), [out_key, tkey], ['rb'])
            S.op('dve', lambda e: e.tensor_tensor(out=rb[:, :, 16:32], in0=st3[:, :, 0:16], in1=pp, op=ALU.mult), [out_key, tkey], ['rb'])
            S.op('dve', lambda e: e.tensor_tensor(out=st3[:, :, 0:32], in0=ra, in1=rb, op=ALU.add), ['ra', 'rb'], [out_key])

        def proj(ps_i, hT_, hT_key, wi):
            wv = v3(W[wi][:, :], 16)
            for k in range(KC):
                S.op('pe', lambda e, k=k: e.matmul(PS[ps_i][:, :], lhsT=hT_[:, k, :], rhs=wv[:, k, :], start=(k == 0), stop=(k == KC - 1)),
                     [hT_key, WK[wi]], [PSK[ps_i]])

        def to_T(src_bf, src_key, dst, dst_key, bank):
            bk = psb(bank)
            for h in range(4):
                S.op('pe', lambda e, h=h: e.transpose(out=bk[:, h * 128:(h + 1) * 128], in_=src_bf[:, h * 128:(h + 1) * 128], identity=identb),
                     [src_key, 'identb'], [PSK[bank]])
            S.op('act', lambda e: e.copy(out=dst, in_=v3(bk[:, 0:512], 4)), [PSK[bank]], [dst_key])

        allblocks = []
        for g in range(3):
            for bi_, (r, b, kind) in enumerate(group_blocks(g) + [(0, 0, 'sample')]):
                allblocks.append((g, bi_, r, b, kind))
        loadidx = {}
        li = 0
        for idx, (g, bi_, r, b, kind) in enumerate(allblocks):
            if kind != 'sample':
                loadidx[idx] = li
                li += 1

        def issue_xload(idx):
            g_, bi__, r_, b_, kind_ = allblocks[idx]
            d_ = GROUPS[g_][1]
            xi_ = loadidx[idx] % 2
            e0 = r_ + d_ * 128 * b_
            S.dma('sp', xblk[xi_], xext[e0:e0 + d_ * 127 + 1:d_, :], writes=['xblk%d' % xi_])

        def next_load(idx):
            for j in range(idx + 1, len(allblocks)):
                if allblocks[j][4] != 'sample':
                    return j
            return None

        NB = len(allblocks)
        tab2 = [tab, v3(af(X_TAB2, 1024), 16)]

        def blk_src(idx):
            g, bi_, r, b, kind = allblocks[idx]
            if kind == 'sample':
                return xsb, 'xsb'
            lx = loadidx[idx] % 2
            return xblk[lx], 'xblk%d' % lx

        def do_norm(idx):
            g, bi_, r, b, kind = allblocks[idx]
            src, skey = blk_src(idx)
            norm_tile(src, skey, hblk, 'hblk')
            if kind != 'sample':
                nl = next_load(idx)
                if nl is not None:
                    issue_xload(nl)
            if bi_ >= 4:
                issue_copy_piece()
                issue_copy_piece()

        def do_T(idx):
            xi = idx % 2
            transposes(hblk, 'hblk', lambda c0: hTb[xi][:, c0:c0 + 8, :], 'hTb%d' % xi)

        def do_block(idx, cur):
            g, bi_, r, b, kind = allblocks[idx]
            wb, d = GROUPS[g]
            xi = idx % 2
            hkey = 'hTb%d' % xi
            kk = idx % 2
            tkey = 'tabS' if kind == 'sample' else 'tab%d' % (g % 2)
            tb = tabS if kind == 'sample' else tab2[g % 2][:, bi_, :]
            nxt = idx + 1 if idx + 1 < NB else None
            if nxt is not None:
                g2_, bi2_ = allblocks[nxt][0], allblocks[nxt][1]
                if bi2_ == 0:
                    S.dma('sp', tab2[g2_ % 2], tabs_d[g2_], writes=['tab%d' % (g2_ % 2)])
                do_norm(nxt)
            has_q = kind != 'kv'
            if has_q:
                proj(7, hTb[xi], hkey, 2)
            proj(0, hTb[xi], hkey, 0)
            proj(1, hTb[xi], hkey, 1)
            if kind == 'sample':
                k_st, k_key = ks_all[:, g, :], 'ks_all'
                v_st, v_key = vs_all[:, g, :], 'vs_all'
            else:
                k_st, k_key = kst[kk], 'kst%d' % kk
                v_st, v_key = vst[kk], 'vst%d' % kk
            if kind == 'sample':
                rope(7, tb, qs_all[:, g, :], 'qs_all', tkey)
                rope(0, tb, k_st, k_key, tkey)
                S.op('act', lambda e: e.copy(out=v_st, in_=PS[1][:, :]), [PSK[1]], [v_key])
                for s_ in range(4):
                    S.dma('sp', kso[g][s_, wb - 4:wb, :], ks_all[32 * s_:32 * s_ + 4, g, :], reads=['ks_all'], writes=['kso%d' % g])
                    S.dma('sp', vso[g][s_, wb - 4:wb, :], vs_all[32 * s_:32 * s_ + 4, g, :], reads=['vs_all'], writes=['vso%d' % g])
                if nxt is not None:
                    do_T(nxt)
                return
            if has_q:
                rope(7, tb, qst, 'qst', tkey)
                S.op('dve', lambda e: e.tensor_copy(out=qb, in_=qst), ['qst'], ['qb'])
            rope(0, tb, k_st, k_key, tkey)
            S.op('dve', lambda e: e.tensor_copy(out=kb, in_=k_st), [k_key], ['kb'])
            S.op('act', lambda e: e.copy(out=v_st, in_=PS[1][:, :]), [PSK[1]], [v_key])
            S.op('dve', lambda e: e.tensor_copy(out=vb[cur], in_=v_st), [v_key], ['vb%d' % cur])
            if has_q:
                to_T(qb, 'qb', qT, 'qT', 5)
            to_T(kb, 'kb', kT[cur], 'kT%d' % cur, 6)
            if g == 0 and b == 15:
                S.dma('sp', kp[0], k_st, reads=[k_key], writes=['kp0'])
                S.dma('sp', vp[0], v_st, reads=[v_key], writes=['vp0'])
            if g == 1 and b == 3:
                S.dma('sp', kp[1][r:r + 4 * 127 + 1:4, :], k_st, reads=[k_key], writes=['kp1'])
                S.dma('sp', vp[1][r:r + 4 * 127 + 1:4, :], v_st, reads=[v_key], writes=['vp1'])
            if g == 2:
                S.dma('sp', kp[2][r:r + 16 * 63 + 1:16, :], k_st[64:128, :], reads=[k_key], writes=['kp2'])
                S.dma('sp', vp[2][r:r + 16 * 63 + 1:16, :], v_st[64:128, :], reads=[v_key], writes=['vp2'])
            if nxt is not None:
                do_T(nxt)
            if not has_q:
                return
            if kind == 'g2':
                keyblocks = [(cur, 3)]
                q0, nq = 64, 64
            elif kind == 'halo':
                keyblocks = [(cur ^ 1, 2), (cur, 0)]
                q0, nq = 0, 128
            else:
                keyblocks = [(cur ^ 1, 1), (cur, 0)]
                q0, nq = 0, 128
            sbanks = (4, 5)
            for ki, (kbuf, mi) in enumerate(keyblocks):
                sb_ = sbanks[ki]
                for h in range(4):
                    S.op('pe', lambda e: e.matmul(PS[sb_][:, h * nq:(h + 1) * nq], lhsT=kT[kbuf][:, h, :], rhs=qT[:, h, q0:q0 + nq], start=True, stop=False),
                         ['kT%d' % kbuf, 'qT'], [PSK[sb_]])
                    S.op('pe', lambda e: e.matmul(PS[sb_][:, h * nq:(h + 1) * nq], lhsT=identb, rhs=mb[:, mi, q0:q0 + nq], start=False, stop=True),
                         ['identb', 'mb'], [PSK[sb_]])
                S.op('act', lambda e: e.activation(out=pT[ki][:, 0:4 * nq], in_=PS[sb_][:, 0:4 * nq], func=AF.Exp, scale=SCALE),
                     [PSK[sb_]], ['pT%d' % ki])
            nkb = len(keyblocks)
            OBK, DBK = 0, 1
            for h in range(4):
                for ki, (kbuf, mi) in enumerate(keyblocks):
                    S.op('pe', lambda e: e.matmul(PS[OBK][:, h * nq:(h + 1) * nq], lhsT=vb[kbuf][:, h * 128:(h + 1) * 128], rhs=pT[ki][:, h * nq:(h + 1) * nq], start=(ki == 0), stop=(ki == nkb - 1)),
                         ['vb%d' % kbuf, 'pT%d' % ki], [PSK[OBK]])
            for ki in range(nkb):
                S.op('pe', lambda e: e.matmul(PS[DBK][:, 0:4 * nq], lhsT=onesb, rhs=pT[ki][:, 0:4 * nq], start=(ki == 0), stop=(ki == nkb - 1)),
                     ['onesb', 'pT%d' % ki], [PSK[DBK]])
            if g == 0:
                cols = slice(128 * (b - 8), 128 * (b - 8) + 128)
            elif g == 1:
                c0 = r + 512 * (b - 2)
                cols = slice(c0, c0 + 4 * 127 + 1, 4)
            else:
                cols = slice(r, r + 16 * 63 + 1, 16)
            po = v3(PS[OBK][:, 0:4 * nq], 4)
            pd = v3(PS[DBK][:, 0:4 * nq], 4)
            nsl = num[:, :, cols]
            dsl = den[:, :, cols]
            if g == 0:
                S.op('act', lambda e: e.copy(out=nsl, in_=po), [PSK[OBK]], ['num'])
                S.op('dve', lambda e: e.tensor_copy(out=dsl, in_=pd), [PSK[DBK]], ['den'])
            else:
                S.op('dve', lambda e: e.tensor_tensor(out=nsl, in0=po, in1=nsl, op=ALU.add), [PSK[OBK], 'num'], ['num'])
                S.op('dve', lambda e: e.tensor_tensor(out=dsl, in0=pd, in1=dsl, op=ALU.add), [PSK[DBK], 'den'], ['den'])

        if stop_after != 'setup':
            issue_xload(0)
            S.dma('sp', tab2[0], tabs_d[0], writes=['tab0'])
            do_norm(0)
            do_T(0)
            cur = 0
            for idx, (g, bi_, r, b, kind) in enumerate(allblocks):
                if bi_ == 0:
                    wload(2, w_in[:, O1 + 512 * g:O1 + 512 * (g + 1)], 16, 512)
                    wload(0, w_in[:, O2 + 512 * g:O2 + 512 * (g + 1)], 16, 512)
                    wload(1, w_in[:, O3 + 512 * g:O3 + 512 * (g + 1)], 16, 512)
                cur ^= 1
                do_block(idx, cur)

        while copy_pieces:
            issue_copy_piece()
        if stop_after not in ('setup',):
            S.op('act', lambda e: e.activation(out=den[:, :, :], in_=den[:, :, :], func=AF.Ln), ['den'], ['den'])
            S.op('act', lambda e: e.activation(out=den[:, :, :], in_=den[:, :, :], func=AF.Exp, scale=-1.0), ['den'], ['den'])
            S.op('dve', lambda e: e.tensor_tensor(out=oT[:, :, 0:1024], in0=num[:, :, :], in1=den[:, :, :], op=ALU.mult), ['num', 'den'], ['oT'])
        dbg('ks_all', ks_all[:, :, :], 'ks_all')
        dbg('qs_all', qs_all[:, :, :], 'qs_all')
        dbg('oT1', oT[:, :, :], 'oT')

        if stop_after in ('setup', 'A1'):
            S.barrier(('sp',))
            S.emit()
            return nc, dbg_outs

        S.barrier()
        o = A2_BASE
        qs_bf = v3(ab(o, 768), 3); o += 768
        qbn2 = [v3(af(o, 2048), 4), v3(af(o + 2048, 2048), 4)]; o += 4096
        prod2 = [af(o, 2048), af(o + 2048, 2048)]; o += 4096
        prod = prod2[0]
        prodV = ab(o, 1024); o += 1024
        indb = ab(o, 2); o += 2
        Kc2 = [v3(af(o, 2048), 4), v3(af(o + 2048, 2048), 4)]; o += 4096
        Vc2 = [v3(af(o, 2048), 4), v3(af(o + 2048, 2048), 4)]; o += 4096
        Pn_all = v3(af(o, 48), 3); o += 48
        Sc = af(o, 16); o += 16
        Pc = af(o, 16); o += 16
        rd = af(o, 16); o += 16
        o = (o + 63) // 64 * 64
        pvn_all = v3(ab(o, 3072), 3); o += 3072
        osm = af(o, 2048); o += 2048
        osn = ab(o, 1024); o += 1024
        assert o <= X_OT, o

        S.op('dve', lambda e: e.tensor_copy(out=qs_bf, in_=qs_all[:, :, :]), ['qs_all'], ['qs_bf'])
        S.op('dve', lambda e: e.tensor_copy(out=indb, in_=ind), ['ind'], ['indb'])

        def bc_i(ap2d):
            return mkap(ap2d, [[0, 4], [1, 512]])

        def as_ihd(ap, istep):
            return mkap(ap, [[istep, 4], [128, 4], [1, 128]])

        def p_bc(ap16):
            return mkap(ap16, [[4, 4], [1, 4], [0, 128]])

        for g in range(3):
            qbn = qbn2[g % 2]
            qk = 'qbn%d' % (g % 2)
            for i in range(4):
                bk = i % 2
                S.op('pe', lambda e: e.matmul(PS[bk][:, :], lhsT=selb[:, i, :], rhs=qs_bf[:, g, :], start=True, stop=True),
                     ['selb', 'qs_bf'], [PSK[bk]])
                S.op('act', lambda e: e.copy(out=qbn[:, i, :], in_=PS[bk][:, :]), [PSK[bk]], [qk])
            prod3 = v3(prod, 4)
            S.op('dve', lambda e: e.tensor_tensor(out=prod3, in0=bc_i(ks_all[:, g, :]), in1=qbn[:, :, :], op=ALU.mult), ['ks_all', qk], ['prod'])
            S.op('dve', lambda e: e.tensor_reduce(out=Sc, in_=v3(prod, 16), axis=AX.X, op=ALU.add), ['prod'], ['Sc'])
            S.op('act', lambda e: e.activation(out=Pn_all[:, g, :], in_=Sc, func=AF.Exp, scale=SCALE), ['Sc'], ['Pn_all'])
            mrow = 1 if g == 0 else 2
            S.op('dve', lambda e: e.tensor_tensor(out=Pn_all[:, g, :], in0=Pn_all[:, g, :], in1=smask[:, mrow, :], op=ALU.mult), ['Pn_all', 'smask'], ['Pn_all'])
            S.op('dve', lambda e: e.tensor_tensor(out=as_ihd(pvn_all[:, g, :], 512), in0=as_ihd(vs_all[:, g, :], 0), in1=p_bc(Pn_all[:, g, :]), op=ALU.mult),
                 ['vs_all', 'Pn_all'], ['pvn_all'])

        iters = [(s_, g) for s_ in range(4) for g in range(3)]

        def stage_x(it):
            s_, g = iters[it]
            wb, d = GROUPS[g]
            bi = it % 2
            Kc, Vc, qbn = Kc2[bi], Vc2[bi], qbn2[bi]
            kkey, vkey, qk = 'Kc%d' % bi, 'Vc%d' % bi, 'qbn%d' % bi
            if g == 0:
                S.dma('sp', Kc[:, 0, :], ck[0][s_], writes=[kkey])
                S.dma('sp', Vc[:, 0, :], cv[0][s_], writes=[vkey])
            else:
                S.dma('sp', Kc[:, :, :], ck[g][s_].rearrange("(p q) c -> p q c", q=d)[:, 0:4, :], writes=[kkey])
                S.dma('sp', Vc[:, :, :], cv[g][s_].rearrange("(p q) c -> p q c", q=d)[:, 0:4, :], writes=[vkey])
            for i in range(4):
                bk = i % 2
                col = identb[:, 32 * s_ + i:32 * s_ + i + 1]
                selc = mkap(col, [[0, 128]])
                S.op('pe', lambda e: e.matmul(PS[bk][:, :], lhsT=selc, rhs=qs_bf[:, g, :], start=True, stop=True),
                     ['identb', 'qs_bf'], [PSK[bk]])
                S.op('act', lambda e: e.copy(out=qbn[:, i, :], in_=PS[bk][:, :]), [PSK[bk]], [qk])

        def stage_y(it):
            s_, g = iters[it]
            bi = it % 2
            Kc, Vc, qbn = Kc2[bi], Vc2[bi], qbn2[bi]
            kkey, vkey, qk = 'Kc%d' % bi, 'Vc%d' % bi, 'qbn%d' % bi
            if g == 0:
                kview, vview = bc_i(Kc[:, 0, :]), as_ihd(Vc[:, 0, :], 0)
            else:
                kview, vview = Kc[:, :, :], as_ihd(Vc[:, 0, :], 512)
            prod_ = prod2[bi]
            prod3 = v3(prod_, 4)
            S.op(A2_PROD_ENG[0], lambda e: e.tensor_tensor(out=prod3, in0=kview, in1=qbn[:, :, :], op=ALU.mult), [kkey, qk], ['prod%d' % bi])
            S.op('dve', lambda e: e.tensor_reduce(out=Sc, in_=v3(prod_, 16), axis=AX.X, op=ALU.add), ['prod%d' % bi], ['Sc'])
            S.op('act', lambda e: e.activation(out=Pc, in_=Sc, func=AF.Exp, scale=SCALE), ['Sc'], ['Pc'])
            if g == 0:
                S.op('dve', lambda e: e.tensor_tensor(out=Pc, in0=Pc, in1=smask[:, 0, :], op=ALU.mult), ['Pc', 'smask'], ['Pc'])
            S.op('dve', lambda e: e.tensor_tensor(out=as_ihd(prodV, 512), in0=vview, in1=p_bc(Pc), op=ALU.mult), [vkey, 'Pc'], ['prodV'])
            for j in range(4):
                S.op('pe', lambda e: e.matmul(PS[2 + j][0:1, :], lhsT=onesb[:, 0:1], rhs=prodV[:, 512 * j:512 * (j + 1)], start=(g == 0), stop=False),
                     ['onesb', 'prodV'], [PSK[2 + j]])
                S.op('pe', lambda e: e.matmul(PS[2 + j][0:1, :], lhsT=indb[:, s_:s_ + 1], rhs=pvn_all[:, g, 512 * j:512 * (j + 1)], start=False, stop=(g == 2)),
                     ['indb', 'pvn_all'], [PSK[2 + j]])
            S.op('pe', lambda e: e.matmul(PS[6][0:1, 0:16], lhsT=onesf[:, 0:1], rhs=Pc, start=(g == 0), stop=False), ['onesf', 'Pc'], [PSK[6]])
            S.op('pe', lambda e: e.matmul(PS[6][0:1, 0:16], lhsT=ind[:, s_:s_ + 1], rhs=Pn_all[:, g, :], start=False, stop=(g == 2)), ['ind', 'Pn_all'], [PSK[6]])
            if g != 2:
                return
            for j in range(4):
                S.op('act', lambda e: e.copy(out=osm[0:1, 512 * j:512 * (j + 1)], in_=PS[2 + j][0:1, :]), [PSK[2 + j]], ['osm'])
            S.op('dve', lambda e: e.reciprocal(out=rd[0:1, :], in_=PS[6][0:1, 0:16]), [PSK[6]], ['rd'])
            S.op('dve', lambda e: e.tensor_tensor(out=v3(osn[0:1, :], 16), in0=v3(osm[0:1, :], 16), in1=mkap(rd[0:1, :], [[1, 16], [0, 128]]), op=ALU.mult),
                 ['osm', 'rd'], ['osn'])
            for c in range(16):
                S.op('pe', lambda e: e.matmul(PS[7][:, c:c + 1], lhsT=osn[0:1, c * 128:(c + 1) * 128], rhs=onesb[0:1, 0:1], start=True, stop=True),
                     ['osn', 'onesb'], [PSK[7]])
            dstv = oT[:, :, 1024 + 4 * s_:1024 + 4 * s_ + 4].rearrange("p h i -> p i h")
            S.op('act', lambda e: e.copy(out=dstv, in_=PS[7][:, 0:16].rearrange("p (i h) -> p i h", i=4)), [PSK[7]], ['oT'])

        stage_x(0)
        for it in range(len(iters)):
            if it + 1 < len(iters):
                stage_x(it + 1)
            stage_y(it)
        dbg('oT2', oT[:, :, :], 'oT')
        if stop_after == 'A2':
            S.barrier(('sp',))
            S.emit()
            return nc, dbg_outs

        S.barrier()
        yT = v3(af(OB, 8320), 8)
        hT = v3(ab(OB + 9216, 8560), 16)
        cT = v3(ab(OB + 19456, 4160), 8)
        o = OB + 19456
        xsb3 = af(o, 2048); o += 2048
        xb3 = [af(o, 2048), af(o + 2048, 2048)]; o += 4096
        hblk3 = ab(o, 1024); o += 1024
        assert o == OB + 26624
        LN_BASE = OB + 26624
        u2 = af(o, 384); o += 384
        sg = af(o, 512); o += 512
        stT = v3(af(o, 960), 8); o += 960
        o = OB + 30736
        cpo2 = af(o, 1024); o += 1024
        cso2 = af(o, 1024); o += 1024
        assert o <= X_OT

        S.dma('sp', xsb3[0:16, :], xs_d, writes=['xsb3'])
        straw = xb3[1][0:120, 0:1024]
        S.dma('sp', straw, state_d, writes=['xb3_1'])
        for c in range(8):
            bk = 4 + (c // 4)
            S.op('pe', lambda e, c=c, bk=bk: e.transpose(out=PS[bk][:, (c % 4) * 120:(c % 4) * 120 + 120], in_=straw[:, c * 128:(c + 1) * 128], identity=identf[0:120, 0:120]),
                 ['xb3_1', 'identf'], [PSK[bk]])
        for hh in range(2):
            S.op('act', lambda e, hh=hh: e.copy(out=stT[:, 4 * hh:4 * hh + 4, :], in_=v3(PS[4 + hh][:, 0:480], 4)), [PSK[4 + hh]], ['stT'])
        def glu_views(hh):
            wa_ = W[0][:, 4096 * hh:4096 * (hh + 1)].rearrange("p (k n) -> p k n", k=16)
            wb_ = W[1][:, 4096 * hh:4096 * (hh + 1)].rearrange("p (k n) -> p k n", k=16)
            return wa_, wb_

        def glu_load(ph):
            hh = ph % 2
            wa_, wb_ = glu_views(hh)
            S.dma('pool', wa_, w_in[:, 256 * ph:256 * (ph + 1)].rearrange("(k p) n -> p k n", p=128), writes=['W0h%d' % hh])
            S.dma('pool', wb_, w_in[:, 1024 + 256 * ph:1024 + 256 * (ph + 1)].rearrange("(k p) n -> p k n", p=128), writes=['W1h%d' % hh])

        glu_load(0)
        glu_load(1)
        hb3 = [hblk3, ab(LN_BASE + 1856, 1024)]

        def a3_src(t):
            if t == 8:
                return xsb3, 'xsb3'
            xi = 1 if t == 9 else t % 2
            return xb3[xi], 'xb3_%d' % xi

        A3NP = [128] * 8 + [16, 30]
        A3C0 = [128 * t for t in range(8)] + [1024, 1040]

        def a3_ld(t):
            if t == 8:
                return
            dst, key = a3_src(t)
            if t == 9:
                S.dma('sp', dst[0:30, :], xext[994:1024, :], writes=[key])
            else:
                S.dma('sp', dst, xext[1024 + 128 * t:1024 + 128 * (t + 1), :], writes=[key])

        def a3_norm(t):
            src, skey = a3_src(t)
            norm_tile(src, skey, hb3[t % 2], 'hblk3_%d' % (t % 2), np_=A3NP[t])

        a3_ld(0)
        a3_ld(1)
        a3_norm(0)
        for t in range(10):
            if t + 2 < 10:
                a3_ld(t + 2)
            if t + 1 < 10:
                a3_norm(t + 1)
            transposes(hb3[t % 2], 'hblk3_%d' % (t % 2), lambda c0, t=t: hT[:, c0:c0 + 8, A3C0[t]:A3C0[t] + A3NP[t]], 'hT', np_=A3NP[t])
        dbg('hT', hT[:, :, :], 'hT')

        TG3 = [(0, 357), (357, 357), (714, 356)]
        S.barrier()
        o = OB + 19456
        dg = [v3(ab(o, 1984), 31), v3(ab(o + 1984, 1984), 31)]; o += 3968
        ubb = [ab(o, 528), ab(o + 528, 528)]; o += 1056
        ucb = [v3(ab(o, 68), 4), v3(ab(o + 68, 68), 4)]; o += 136
        assert o <= OB + 26624
        for ph in range(4):
            if ph >= 1 and ph + 1 < 4:
                glu_load(ph + 1)
            gh = ph % 2
            wa, wbb = glu_views(gh)
            for n4 in range(2):
                n = 2 * ph + n4
                ui = n % 2
                ukey = 'ubb%d' % ui
                uckey = 'ucb%d' % ui
                dkey_ = 'dg%d' % ui
                S.op('dve', lambda e: e.tensor_tensor(out=dg[ui][:, :, :], in0=mkap(identb, [[0, 31], [1, 128]]),
                                                      in1=mkap(vecT[:, R_WDW + n:R_WDW + n + 1], [[8, 31], [0, 128]]), op=ALU.mult),
                     ['identb', 'vecT'], [dkey_])
                S.op('act', lambda e: e.copy(out=ucb[ui][:, :, 0:30], in_=stT[:, n, :].rearrange("p (s r) -> p s r", r=30)), ['stT'], [uckey])
                for ti, (c0, nc_) in enumerate(TG3):
                    pa = 0 if (ti % 2 == 0) else 7
                    pb = 1
                    for k in range(KC):
                        S.op('pe', lambda e: e.matmul(PS[pa][:, 0:nc_], lhsT=wa[:, k, n4 * 128:(n4 + 1) * 128], rhs=hT[:, k, c0:c0 + nc_], start=(k == 0), stop=(k == KC - 1)),
                             ['hT', 'W0h%d' % gh], [PSK[pa]])
                    for k in range(KC):
                        S.op('pe', lambda e: e.matmul(PS[pb][:, 0:nc_], lhsT=wbb[:, k, n4 * 128:(n4 + 1) * 128], rhs=hT[:, k, c0:c0 + nc_], start=(k == 0), stop=(k == KC - 1)),
                             ['hT', 'W1h%d' % gh], [PSK[pb]])
                    S.op('act', lambda e: e.activation(out=sg[:, 0:nc_], in_=PS[pb][:, 0:nc_], func=AF.Sigmoid, bias=vcol(R_BGLU + 8 + n)),
                         [PSK[pb], 'vecT'], ['sg'])
                    if ti < 2:
                        dst, dkey = ubb[ui][:, 30 + c0:30 + c0 + nc_], ukey
                    else:
                        dst, dkey = u2[:, 0:nc_], 'u2'
                    S.op('dve', lambda e: e.scalar_tensor_tensor(out=dst, in0=PS[pa][:, 0:nc_], scalar=vcol(R_BGLU + n), in1=sg[:, 0:nc_], op0=ALU.add, op1=ALU.mult),
                         [PSK[pa], 'sg', 'vecT'], [dkey])
                S.op('act', lambda e: e.copy(out=ubb[ui][:, 30 + 714:30 + 1024], in_=u2[:, 0:310]), ['u2'], [ukey])
                S.op('act', lambda e: e.copy(out=ucb[ui][:, :, 30:34], in_=u2[:, 310:326].rearrange("p (s j) -> p s j", j=4)), ['u2'], [uckey])
                S.op('dve', lambda e: e.tensor_scalar(out=ubb[ui][:, 0:30], in0=u2[:, 326:356], scalar1=hv, scalar2=None, op0=ALU.mult), ['u2', 'hv'], [ukey])
                S.op('pe', lambda e: e.transpose(out=PS[4][:, 0:128], in_=u2[:, 182:310], identity=identf), ['u2', 'identf'], [PSK[4]])
                S.op('act', lambda e: e.copy(out=cpo2[:, n * 128:(n + 1) * 128], in_=PS[4][:, 0:128]), [PSK[4]], ['cpo2'])
                S.op('pe', lambda e: e.transpose(out=PS[5][0:16, 0:128], in_=u2[:, 310:326], identity=identf), ['u2', 'identf'], [PSK[5]])
                S.op('act', lambda e: e.copy(out=cso2[0:16, n * 128:(n + 1) * 128], in_=PS[5][0:16, 0:128]), [PSK[5]], ['cso2'])
                ysd = yT[:, n, 1024:1040].rearrange("p (s j) -> p s j", j=4)
                for j in range(31):
                    S.op('pe', lambda e: e.matmul(PS[2][:, :], lhsT=dg[ui][:, j, :], rhs=ubb[ui][:, j:j + 512], start=(j == 0), stop=(j == 30)),
                         [dkey_, ukey], [PSK[2]])
                    S.op('pe', lambda e: e.matmul(PS[3][:, :], lhsT=dg[ui][:, j, :], rhs=ubb[ui][:, 512 + j:512 + j + 512], start=(j == 0), stop=(j == 30)),
                         [dkey_, ukey], [PSK[3]])
                    S.op('pe', lambda e: e.matmul(PS[6][:, 0:16].rearrange("p (s j) -> p s j", j=4), lhsT=dg[ui][:, j, :], rhs=ucb[ui][:, :, j:j + 4], start=(j == 0), stop=(j == 30)),
                         [dkey_, uckey], [PSK[6]])
                S.op('act', lambda e: e.activation(out=yT[:, n, 0:512], in_=PS[2][:, :], func=AF.Identity, bias=vcol(R_BDW + n)), [PSK[2], 'vecT'], ['yT'])
                S.op('dve', lambda e: e.tensor_scalar(out=yT[:, n, 512:1024], in0=PS[3][:, :], scalar1=vcol(R_BDW + n), scalar2=None, op0=ALU.add), [PSK[3], 'vecT'], ['yT'])
                S.op('dve', lambda e: e.tensor_scalar(out=ysd, in0=PS[6][:, 0:16].rearrange("p (s j) -> p s j", j=4), scalar1=vcol(R_BDW + n), scalar2=None, op0=ALU.add), [PSK[6], 'vecT'], ['yT'])
        S.dma('sp', convp_d, cpo2[98:128, :], reads=['cpo2'], writes=['convp'])
        for s in range(4):
            S.dma('sp', convs_d[s, 26:30, :], cso2[4 * s:4 * s + 4, :], reads=['cso2'], writes=['convs'])
        dbg('yT', yT[:, :, :], 'yT')

        def a4_views(hh):
            wgc = W[0][:, 4096 * hh:4096 * (hh + 1)].rearrange("p (k n) -> p k n", k=16)
            wga = W[1][:, 4096 * hh:4096 * (hh + 1)].rearrange("p (k n) -> p k n", k=16)
            w2a = W[2][:, 3072 * hh:3072 * hh + 2048].rearrange("p (k n) -> p k n", k=8)
            w2b = W[2][:, 3072 * hh + 2048:3072 * (hh + 1)].rearrange("p (k n) -> p k n", k=4)
            return wgc, wga, w2a, w2b

        def a4_load(jh):
            hh = jh % 2
            wgc, wga, w2a, w2b = a4_views(hh)
            cs = slice(256 * jh, 256 * (jh + 1))
            S.dma('pool', wgc, w_in[:, O4 + 256 * jh:O4 + 256 * (jh + 1)].rearrange("(k p) n -> p k n", p=128), writes=['W0h%d' % hh])
            S.dma('pool', wga, w_in[:, O5 + 256 * jh:O5 + 256 * (jh + 1)].rearrange("(k p) n -> p k n", p=128), writes=['W1h%d' % hh])
            S.dma('pool', w2a, w_pw[:, cs].rearrange("(k p) n -> p k n", p=128), writes=['W2ah%d' % hh])
            S.dma('pool', w2b, w_o[:, cs].rearrange("(k p) n -> p k n", p=128), writes=['W2bh%d' % hh])

        a4_load(0)
        a4_load(1)
        S.barrier()
        o = LN_BASE
        LW = 352
        ysq = [af(o, LW), af(o + LW, LW)]; o += 2 * LW
        mean3 = [af(o + i * LW, LW) for i in range(3)]; o += 3 * LW
        msq = af(o, LW); o += LW
        lrs3 = [af(o + i * LW, LW) for i in range(3)]; o += 3 * LW
        tt = [af(o, LW), af(o + LW, LW)]; o += 2 * LW
        assert o <= OB + 30736
        TGN = [(0, 347), (347, 347), (694, 346)]
        for ti, (c0, nc_) in enumerate(TGN):
            b0, b1 = 2 * ti, 2 * ti + 1
            for n in range(8):
                S.op('pe', lambda e: e.matmul(PS[b0][:, 0:nc_], lhsT=onesf, rhs=yT[:, n, c0:c0 + nc_], start=(n == 0), stop=(n == 7)),
                     ['yT', 'onesf'], [PSK[b0]])
                qi = n % 2
                S.op('act', lambda e: e.activation(out=ysq[qi][:, 0:nc_], in_=yT[:, n, c0:c0 + nc_], func=AF.Square), ['yT'], ['ysq%d' % qi])
                S.op('pe', lambda e: e.matmul(PS[b1][:, 0:nc_], lhsT=onesf, rhs=ysq[qi][:, 0:nc_], start=(n == 0), stop=(n == 7)),
                     ['ysq%d' % qi, 'onesf'], [PSK[b1]])
        for ti, (c0, nc_) in enumerate(TGN):
            b0, b1 = 2 * ti, 2 * ti + 1
            mean, lrs = mean3[ti], lrs3[ti]
            mk, lk = 'mean%d' % ti, 'lrs%d' % ti
            S.op('act', lambda e: e.mul(out=mean[:, 0:nc_], in_=PS[b0][:, 0:nc_], mul=1.0 / 1024), [PSK[b0]], [mk])
            S.op('dve', lambda e: e.tensor_tensor(out=msq[:, 0:nc_], in0=mean[:, 0:nc_], in1=mean[:, 0:nc_], op=ALU.mult), [mk], ['msq'])
            S.op('dve', lambda e: e.scalar_tensor_tensor(out=lrs[:, 0:nc_], in0=PS[b1][:, 0:nc_], scalar=1.0 / 1024, in1=msq[:, 0:nc_], op0=ALU.mult, op1=ALU.subtract),
                 [PSK[b1], 'msq'], [lk])
            S.op('act', lambda e: e.activation(out=lrs[:, 0:nc_], in_=lrs[:, 0:nc_], func=AF.Ln, bias=eps_ln), [lk, 'epsl'], [lk])
            S.op('act', lambda e: e.activation(out=lrs[:, 0:nc_], in_=lrs[:, 0:nc_], func=AF.Exp, scale=-0.5), [lk], [lk])
        for ti, (c0, nc_) in enumerate(TGN):
            mean, lrs = mean3[ti], lrs3[ti]
            mk, lk = 'mean%d' % ti, 'lrs%d' % ti
            for n in range(8):
                qi = n % 2
                S.op('dve', lambda e: e.tensor_tensor(out=tt[qi][:, 0:nc_], in0=yT[:, n, c0:c0 + nc_], in1=mean[:, 0:nc_], op=ALU.subtract),
                     ['yT', mk], ['tt%d' % qi])
                S.op('dve', lambda e: e.tensor_tensor(out=tt[qi][:, 0:nc_], in0=tt[qi][:, 0:nc_], in1=lrs[:, 0:nc_], op=ALU.mult),
                     ['tt%d' % qi, lk], ['tt%d' % qi])
                S.op('act', lambda e: e.activation(out=cT[:, n, c0:c0 + nc_], in_=tt[qi][:, 0:nc_], func=AF.Silu, scale=vcol(R_LNG + n), bias=vcol(R_LNB + n)),
                     ['tt%d' % qi, 'vecT'], ['cT'])
        dbg('cT', cT[:, :, :], 'cT')
        if stop_after == 'A3':
            S.barrier(('sp',))
            S.emit()
            return nc, dbg_outs

        S.barrier()
        mixT = v3(ab(OB, 8320), 16)
        o = OB + 24064
        sgc = [af(o, 512), af(o + 512, 512)]; o += 1024
        sga = [af(o, 512), af(o + 512, 512)]; o += 1024
        t1 = [af(o, 512), af(o + 512, 512)]; o += 1024
        t2 = [af(o, 512), af(o + 512, 512)]; o += 1024
        it = 0

        for jh in range(8):
            if jh >= 1 and jh + 1 < 8:
                a4_load(jh + 1)
            hh = jh % 2
            wgc, wga, w2a, w2b = a4_views(hh)
            for n2 in range(2):
                n = 2 * jh + n2
                for (c0, nc_) in TGN:
                    pb_ = 4 * (it % 2)
                    q = it % 2
                    it += 1
                    wsl = slice(n2 * 128, (n2 + 1) * 128)
                    for k in range(KC):
                        S.op('pe', lambda e: e.matmul(PS[pb_][:, 0:nc_], lhsT=wgc[:, k, wsl], rhs=hT[:, k, c0:c0 + nc_], start=(k == 0), stop=(k == KC - 1)),
                             ['hT', 'W0h%d' % hh], [PSK[pb_]])
                    for k in range(KC):
                        S.op('pe', lambda e: e.matmul(PS[pb_ + 1][:, 0:nc_], lhsT=wga[:, k, wsl], rhs=hT[:, k, c0:c0 + nc_], start=(k == 0), stop=(k == KC - 1)),
                             ['hT', 'W1h%d' % hh], [PSK[pb_ + 1]])
                    for k in range(8):
                        S.op('pe', lambda e: e.matmul(PS[pb_ + 2][:, 0:nc_], lhsT=w2a[:, k, wsl], rhs=cT[:, k, c0:c0 + nc_], start=(k == 0), stop=(k == 7)),
                             ['cT', 'W2ah%d' % hh], [PSK[pb_ + 2]])
                    for k in range(4):
                        S.op('pe', lambda e: e.matmul(PS[pb_ + 3][:, 0:nc_], lhsT=w2b[:, k, wsl], rhs=oT[:, k, c0:c0 + nc_], start=(k == 0), stop=(k == 3)),
                             ['oT', 'W2bh%d' % hh], [PSK[pb_ + 3]])
                    S.op('act', lambda e: e.activation(out=sgc[q][:, 0:nc_], in_=PS[pb_][:, 0:nc_], func=AF.Sigmoid), [PSK[pb_]], ['sgc%d' % q])
                    S.op('act', lambda e: e.activation(out=sga[q][:, 0:nc_], in_=PS[pb_ + 1][:, 0:nc_], func=AF.Sigmoid), [PSK[pb_ + 1]], ['sga%d' % q])
                    S.op('dve', lambda e: e.scalar_tensor_tensor(out=t1[q][:, 0:nc_], in0=PS[pb_ + 2][:, 0:nc_], scalar=vcol(R_BPW + n), in1=sgc[q][:, 0:nc_], op0=ALU.add, op1=ALU.mult),
                         [PSK[pb_ + 2], 'sgc%d' % q, 'vecT'], ['t1%d' % q])
                    S.op('dve', lambda e: e.tensor_tensor(out=t2[q][:, 0:nc_], in0=PS[pb_ + 3][:, 0:nc_], in1=sga[q][:, 0:nc_], op=ALU.mult),
                         [PSK[pb_ + 3], 'sga%d' % q], ['t2%d' % q])
                    S.op('dve', lambda e: e.tensor_tensor(out=mixT[:, n, c0:c0 + nc_], in0=t1[q][:, 0:nc_], in1=t2[q][:, 0:nc_], op=ALU.add),
                         ['t1%d' % q, 't2%d' % q], mxk(c0, nc_))
        dbg('mixT', mixT[:, :, :], list(MK_ALL))
        if stop_after == 'A4':
            S.barrier(('sp',))
            S.emit()
            return nc, dbg_outs

        def a5_buf(jh):
            wi, hh = (jh % 4) // 2, jh % 2
            ws_ = W[wi][:, 4096 * hh:4096 * (hh + 1)].rearrange("p (k n) -> p k n", k=16)
            return ws_, 'W%dh%d' % (wi, hh)

        def a5_load(jh):
            ws_, key_ = a5_buf(jh)
            S.dma('pool', ws_, w_out[:, 256 * jh:256 * (jh + 1)].rearrange("(k p) n -> p k n", p=128), writes=[key_])

        for jh in range(4):
            a5_load(jh)
        S.barrier()
        xacc = v3(af(OB + 9216, 18432), 9)
        o = OB + 27648
        hblk5 = ab(o, 1024); o += 1024
        aq = v3(ab(o, 2080), 4); o += 2304
        sgb = [af(o, 512), af(o + 512, 512)]; o += 1024
        junk = ab(o, 1024); o += 1024
        hblk5b = ab(o, 1024); o += 1024
        assert o <= X_OT + 2304
        XK = ['xacc%d' % t for t in range(9)]
        for t in range(8):
            S.dma('sp', xacc[:, t, :], xext[1024 + 128 * t:1024 + 128 * (t + 1), :], writes=[XK[t]])
        S.dma('sp', xacc[0:16, 8, :], xs_d, writes=[XK[8]])
        TNP = [128] * 8 + [16]
        load_gbc(1)
        it = 0
        hfT = mixT
        hb5 = [hblk5, hblk5b]

        def a5_norm(t):
            norm_tile(xacc[:, t, :], XK[t], hb5[t % 2], 'hblk5_%d' % (t % 2), np_=TNP[t])

        def a5_T(t):
            transposes(hb5[t % 2], 'hblk5_%d' % (t % 2), lambda c0: hfT[:, c0:c0 + 8, 128 * t:128 * t + TNP[t]], 'mixT%d' % t, banks=(4, 5), np_=TNP[t])

        for jh in range(8):
            if jh >= 4:
                a5_load(jh)
            ws, wkey = a5_buf(jh)
            for t in range(9):
                bk = it % 4
                it += 1
                np_ = TNP[t]
                for k in range(KC):
                    S.op('pe', lambda e: e.matmul(PS[bk][0:np_, 0:256], lhsT=mixT[:, k, 128 * t:128 * t + np_], rhs=ws[:, k, :], start=(k == 0), stop=(k == KC - 1)),
                         ['mixT%d' % t, wkey], [PSK[bk]])
                xs_ = xacc[0:np_, t, 256 * jh:256 * (jh + 1)]
                S.op('dve', lambda e: e.tensor_tensor(out=xs_, in0=PS[bk][0:np_, 0:256], in1=xs_, op=ALU.add), [PSK[bk], XK[t]], [XK[t]])
        a5_norm(0)
        for t in range(9):
            if t + 1 < 9:
                a5_norm(t + 1)
            a5_T(t)
        dbg('x1', xacc[:, :, :], list(XK))
        dbg('hfT', hfT[:, :, :], list(MK_ALL))
        if stop_after == 'A5':
            S.barrier(('sp',))
            S.emit()
            return nc, dbg_outs

        def final_norm(t):
            np_ = TNP[t]
            xt_ = xacc[0:np_, t, :]
            S.op('act', lambda e: e.activation(out=junk[0:np_, :], in_=xt_, func=AF.Square, accum_out=ssq[0:np_, :]), [XK[t]], ['junk', 'ssq'])
            S.op('act', lambda e: e.activation(out=rs[0:np_, :], in_=ssq[0:np_, :], func=AF.Ln, scale=1.0 / D, bias=eps_norm[0:np_, :]), ['ssq', 'epsc'], ['rs'])
            S.op('act', lambda e: e.activation(out=rstd[0:np_, :], in_=rs[0:np_, :], func=AF.Exp, scale=-0.5), ['rs'], ['rstd'])
            S.op('dve', lambda e: e.scalar_tensor_tensor(out=xt_, in0=xt_, scalar=rstd[0:np_, :], in1=gbc[0:np_, :], op0=ALU.mult, op1=ALU.mult), [XK[t], 'rstd', 'gbc'], [XK[t]])
            if t < 8:
                S.dma('sp', y_d[128 * t:128 * (t + 1), :], xt_, reads=[XK[t]], writes=['y'])
            else:
                S.dma('sp', ys_d, xt_, reads=[XK[t]], writes=['ys'])

        load_gbc(2)
        it = 0
        itd = 0
        for hs in range(NHS):
            wg = wload(0, w_gate[:, 512 * hs:512 * (hs + 1)], 16, 512)
            wu = wload(1, w_up[:, 512 * hs:512 * (hs + 1)], 16, 512)
            wd = W[2][:, 0:8192].rearrange("p (k n) -> p k n", k=4)
            S.dma('pool', wd, w_down[512 * hs:512 * (hs + 1), :].rearrange("(k p) n -> p k n", p=128), writes=['W2', 'W2b'])
            for n4 in range(4):
                wsl = slice(n4 * 128, (n4 + 1) * 128)
                for (c0, nc_) in TGN:
                    q = it % 2
                    it += 1
                    pg, pu = q, 2 + q
                    for k in range(KC):
                        S.op('pe', lambda e, k=k, pg=pg, c0=c0, nc_=nc_, wsl=wsl: e.matmul(PS[pg][:, 0:nc_], lhsT=wg[:, k, wsl], rhs=hfT[:, k, c0:c0 + nc_], start=(k == 0), stop=(k == KC - 1)),
                             mxk(c0, nc_) + ['W0'], [PSK[pg]])
                    for k in range(KC):
                        S.op('pe', lambda e, k=k, pu=pu, c0=c0, nc_=nc_, wsl=wsl: e.matmul(PS[pu][:, 0:nc_], lhsT=wu[:, k, wsl], rhs=hfT[:, k, c0:c0 + nc_], start=(k == 0), stop=(k == KC - 1)),
                             mxk(c0, nc_) + ['W1'], [PSK[pu]])
                    S.op('act', lambda e, q=q, pg=pg, nc_=nc_: e.activation(out=sgb[q][:, 0:nc_], in_=PS[pg][:, 0:nc_], func=AF.Silu), [PSK[pg]], ['sgb%d' % q])
                    S.op('dve', lambda e, q=q, pu=pu, n4=n4, c0=c0, nc_=nc_: e.tensor_tensor(out=aq[:, n4, c0:c0 + nc_], in0=PS[pu][:, 0:nc_], in1=sgb[q][:, 0:nc_], op=ALU.mult),
                         [PSK[pu], 'sgb%d' % q], ['aq'])
            for t in range(9):
                for j in range(4):
                    bk = 4 + (itd % 4)
                    itd += 1
                    np_ = TNP[t]
                    for k in range(4):
                        S.op('pe', lambda e: e.matmul(PS[bk][0:np_, :], lhsT=aq[:, k, 128 * t:128 * t + np_], rhs=wd[:, k, 512 * j:512 * (j + 1)], start=(k == 0), stop=(k == 3)),
                             ['aq', 'W2', 'W2b'], [PSK[bk]])
                    xs_ = xacc[0:np_, t, 512 * j:512 * (j + 1)]
                    S.op('dve', lambda e: e.tensor_tensor(out=xs_, in0=PS[bk][0:np_, :], in1=xs_, op=ALU.add), [PSK[bk], XK[t]], [XK[t]])
                if hs == NHS - 1:
                    final_norm(t)
        S.barrier(('sp',))
        S.emit()
        return nc, dbg_outs


def _rope_tab(pos):
    inv = np.power(np.float32(500000.0), -np.arange(0, 32, 2, dtype=np.float32) / np.float32(32)).astype(np.float32)
    ang = pos.astype(np.float32)[:, None] * inv[None, :]
    c = np.cos(ang).astype(np.float32)
    s = np.sin(ang).astype(np.float32)
    return np.concatenate([c, c, -s, s], axis=1).astype(np.float32)


def _consts(half):
    k = np.arange(128)[:, None]
    q = np.arange(128)[None, :]
    own = np.where(k <= q, 0.0, NEG)
    prev = np.where(k >= q, 0.0, NEG)
    prevh = prev if half == 1 else np.full((128, 128), NEG)
    g2 = own.copy()
    if half == 0:
        g2[:64, :] = NEG
    masks = np.stack([own, prev, prevh, g2], axis=1).astype(np.float32)
    p = np.arange(128)[:, None]
    ih = np.arange(16)[None, :]
    i_of = ih // 4
    m0 = (p >= i_of).astype(np.float32)
    j = p % 32
    valid = (j < 4)
    mn0 = (valid & (j <= i_of)).astype(np.float32)
    mn12 = (valid & (j == i_of)).astype(np.float32)
    smask = np.stack([m0, mn0, mn12], axis=1).astype(np.float32)
    sel = np.zeros((128, 4, 128), np.float32)
    m = np.arange(128)
    for i in range(4):
        sel[32 * (m // 32) + i, i, m] = 1.0
    ind = np.zeros((128, 4), np.float32)
    for s in range(4):
        ind[32 * s:32 * s + 32, s] = 1.0
    hv = np.full((128, 1), float(half), np.float32)
    return masks, smask, sel, ind, hv


def _tabs(half):
    T0 = 1024 * half
    out = np.zeros((3, 128, 16, 64), np.float32)
    p = np.arange(128)
    for g in range(3):
        d = GROUPS[g][1]
        for bi, (r, b, kind) in enumerate(group_blocks(g)):
            e = r + d * (128 * b + p)
            pos = np.maximum(T0 - 1024 + e, 0)
            out[g, :, bi, :] = _rope_tab(pos)
    posS = PAST + np.minimum(p % 32, 3)
    tabS = _rope_tab(posS)
    return out, tabS


def make_core_inputs(c, inp):
    b, half = c // 2, c % 2
    xp = inp['x_prompt'][b]
    if half == 1:
        xext = xp
    else:
        xext = np.concatenate([np.zeros((1024, D), np.float32), xp[:1024]], axis=0)
    sl = slice(4 * c, 4 * c + 4)
    m = {}
    m['xext'] = np.ascontiguousarray(xext, dtype=np.float32)
    m['xs'] = np.ascontiguousarray(inp['x_sample'][sl].reshape(16, D))
    m['state'] = np.ascontiguousarray(inp['state_conv'][0, sl].reshape(120, 1024))
    caches = ((inp['cache_k_w128'], inp['cache_v_w128']), (inp['cache_k_w512'], inp['cache_v_w512']),
              (inp['cache_k_w2048'], inp['cache_v_w2048']))
    for g, (wb, d) in enumerate(GROUPS):
        m['ck%d' % g] = np.ascontiguousarray(caches[g][0][0, sl].reshape(4, wb, 512))
        m['cv%d' % g] = np.ascontiguousarray(caches[g][1][0, sl].reshape(4, wb, 512))
    m['w_in'] = inp['w_in'][0]
    m['w_pw'] = inp['w_pw'][0]
    m['w_o'] = inp['w_o_att'][0]
    m['w_out'] = inp['w_out'][0]
    m['w_gate'] = inp['w_gate'][0]
    m['w_up'] = inp['w_up'][0]
    m['w_down'] = inp['w_down'][0]
    vecs = np.zeros((384, 128), np.float32)
    vecs[0:16] = inp['g_mix'][0].reshape(16, 128)
    vecs[16:32] = inp['b_glu'][0].reshape(16, 128)
    vecs[32:40] = inp['b_dw'][0].reshape(8, 128)
    vecs[40:48] = inp['ln_g'][0].reshape(8, 128)
    vecs[48:56] = inp['ln_b'][0].reshape(8, 128)
    vecs[56:72] = inp['b_pw'][0].reshape(16, 128)
    vecs[128:128 + 248] = inp['w_dw'][0].reshape(31 * 8, 128)
    m['vecs'] = vecs
    m['gb'] = np.stack([inp['g_mix'][0], inp['g_ffn'][0], inp['g_final']], axis=0).astype(np.float32)
    tabs, tabS = _tabs(half)
    m['tabs'] = tabs
    m['tabS'] = tabS
    masks, smask, sel, ind, hv = _consts(half)
    m['masks'] = masks
    m['ident'] = np.eye(128, dtype=np.float32)
    m['smask'] = smask
    m['sel'] = sel
    m['ind'] = ind
    m['hv'] = hv
    return m


def assemble(results):
    f32 = np.float32
    y_prompt = np.zeros((4, 2048, D), f32)
    y_sample = np.zeros((32, 4, D), f32)
    conv_p = np.zeros((1, 4, 30, 1024), f32)
    conv_s = np.zeros((1, 32, 30, 1024), f32)
    kps = [np.zeros((1, 4, min(wb, 2048), 4, 128), f32) for wb, _ in GROUPS]
    vps = [np.zeros((1, 4, min(wb, 2048), 4, 128), f32) for wb, _ in GROUPS]
    kss = [np.zeros((1, 32, wb, 4, 128), f32) for wb, _ in GROUPS]
    vss = [np.zeros((1, 32, wb, 4, 128), f32) for wb, _ in GROUPS]
    for c, r in enumerate(results):
        b, half = c // 2, c % 2
        y_prompt[b, 1024 * half:1024 * (half + 1)] = r['y']
        y_sample[4 * c:4 * c + 4] = r['ys'].reshape(4, 4, D)
        conv_s[0, 4 * c:4 * c + 4] = r['convs']
        for g, (wb, d) in enumerate(GROUPS):
            kss[g][0, 4 * c:4 * c + 4] = r['kso%d' % g].reshape(4, wb, 4, 128)
            vss[g][0, 4 * c:4 * c + 4] = r['vso%d' % g].reshape(4, wb, 4, 128)
        kps[2][0, b, 1024 * half:1024 * (half + 1)] = r['kp2'].reshape(1024, 4, 128)
        vps[2][0, b, 1024 * half:1024 * (half + 1)] = r['vp2'].reshape(1024, 4, 128)
        if half == 1:
            conv_p[0, b] = r['convp']
            kps[0][0, b] = r['kp0'].reshape(128, 4, 128)
            vps[0][0, b] = r['vp0'].reshape(128, 4, 128)
            kps[1][0, b] = r['kp1'].reshape(512, 4, 128)
            vps[1][0, b] = r['vp1'].reshape(512, 4, 128)
    return (y_prompt, y_sample, conv_p, conv_s,
            kps[0], kss[0], vps[0], vss[0],
            kps[1], kss[1], vps[1], vss[1],
            kps[2], kss[2], vps[2], vss[2])


_PROG = {}


def kernel(**inputs):
    inp = {k: np.asarray(v) for k, v in inputs.items()}
    if 'nc' not in _PROG:
        _PROG['nc'] = build_program()[0]
    nc = _PROG['nc']
    in_maps = [make_core_inputs(c, inp) for c in range(8)]
    res = run_bass_kernel_spmd(nc, in_maps, core_ids=list(range(8)))
    return assemble(res.results)
```

```python
import contextlib
import numpy as np
import concourse.bass as bass
import concourse.mybir as mybir
from concourse.bass_utils import run_bass_kernel_spmd

F32 = mybir.dt.float32
BF16 = mybir.dt.bfloat16
AF = mybir.ActivationFunctionType
ALU = mybir.AluOpType
AX = mybir.AxisListType

D = 2048
KC = 16
HID = 5632
NHS = 11
O1, O2, O3, O4, O5 = 2048, 3584, 5120, 6656, 8704
GROUPS = ((128, 1), (512, 4), (2048, 16))
SCALE = 128 ** -0.5
NEG = -30000.0
NORM_EPS = 1e-6
LN_EPS = 1e-5
DEBUG_MAXBLK = [10 ** 9]
DEBUG_CUT = [99]
A2_PROD_ENG = ['dve']
CAST_ENG = ['dve']
DEBUG_NOROPE = [0]
NCOL = 1152
PAST = 8192


class _Rec:
    def __init__(self):
        self.call = None

    def __getattr__(self, meth):
        def f(*a, **kw):
            self.call = (meth, a, kw)
            return self
        return f


class Sched:
    COMPUTE = ('pe', 'act', 'dve', 'pool')

    def __init__(self, nc, stack, ring=8):
        self.nc = nc
        self.q = {e: [] for e in ('pe', 'act', 'dve', 'pool', 'sp')}
        self.sems = {}
        self.cnt = {}
        for e in self.COMPUTE:
            self.sems['p_' + e] = stack.enter_context(nc.semaphore('p_' + e))
            self.cnt['p_' + e] = 0
        self.rings = {}
        for qn in ('sp', 'act', 'pool'):
            names = []
            for i in range(ring):
                nm = 'd_%s%d' % (qn, i)
                self.sems[nm] = stack.enter_context(nc.semaphore(nm))
                self.cnt[nm] = 0
                names.append(nm)
            self.rings[qn] = [names, 0]
        self.last_w = {}
        self.readers = {}
        self.waited = {e: {} for e in self.q}
        self.nwaits = 0
        self.ninst = {e: 0 for e in self.q}

    def _deps(self, reads, writes):
        deps = []
        for k in reads:
            t = self.last_w.get(k)
            if t is not None:
                deps.append(t)
        for k in writes:
            t = self.last_w.get(k)
            if t is not None:
                deps.append(t)
            rd = self.readers.get(k)
            if rd:
                deps.extend(rd.items())
        return deps

    def _wait(self, eng, tok, force=False):
        sem, val = tok
        if not force and eng == 'pe' and sem == 'p_pe':
            return
        if self.waited[eng].get(sem, 0) >= val:
            return
        self.waited[eng][sem] = val
        s = self.sems[sem]
        self.q[eng].append(lambda e, s=s, val=val: e.wait_ge(s, val))
        self.nwaits += 1

    def _record(self, tok, reads, writes):
        sem, val = tok
        for k in reads:
            d = self.readers.setdefault(k, {})
            if d.get(sem, 0) < val:
                d[sem] = val
        for k in writes:
            self.last_w[k] = tok
            self.readers[k] = {}

    def op(self, eng, fn, reads=(), writes=()):
        pk = [k for k in reads if isinstance(k, tuple) and k[0] == 'ps']
        if pk:
            reads = [k for k in reads if not (isinstance(k, tuple) and k[0] == 'ps')]
            writes = list(writes) + pk
        for t in self._deps(reads, writes):
            self._wait(eng, t)
        nm = 'p_' + eng
        self.cnt[nm] += 1
        s = self.sems[nm]
        rec = _Rec()
        fn(rec)
        meth, a, kw = rec.call
        self.q[eng].append(lambda e, meth=meth, a=a, kw=kw, s=s: getattr(e, meth)(*a, **kw).then_inc(s, 1))
        tok = (nm, self.cnt[nm])
        self._record(tok, reads, writes)
        self.ninst[eng] += 1
        return tok

    def dma(self, queue, out, in_, reads=(), writes=(), **kw):
        names, idx = self.rings[queue]
        nm = names[idx % len(names)]
        self.rings[queue][1] = idx + 1
        if self.cnt[nm] > 0:
            self._wait(queue, (nm, self.cnt[nm]), force=True)
        for t in self._deps(reads, writes):
            self._wait(queue, t, force=True)
        self.cnt[nm] += 16
        s = self.sems[nm]
        self.q[queue].append(
            lambda e, s=s, out=out, in_=in_, kw=kw: e.dma_start(out=out, in_=in_, **kw).then_inc(s, 16))
        tok = (nm, self.cnt[nm])
        self._record(tok, reads, writes)
        self.ninst[queue] += 1
        return tok

    def barrier(self, engines=('pe', 'act', 'dve', 'pool', 'sp')):
        toks = [(nm, c) for nm, c in self.cnt.items() if c > 0 and not nm.startswith('d_pool')]
        for e in engines:
            for t in toks:
                self._wait(e, t, force=True)

    def emit(self):
        nc = self.nc
        with nc.Block() as block:
            @block.tensor
            def _(e):
                for f in self.q['pe']:
                    f(e)

            @block.scalar
            def _(e):
                for f in self.q['act']:
                    f(e)

            @block.vector
            def _(e):
                for f in self.q['dve']:
                    f(e)

            @block.gpsimd
            def _(e):
                for f in self.q['pool']:
                    f(e)

            @block.sync
            def _(e):
                for f in self.q['sp']:
                    f(e)


def v3(ap, a):
    return ap.rearrange("p (a b) -> p a b", a=a)


def mkap(ap, dims):
    return bass.AP(ap.tensor, ap.offset, [list(ap.ap[0])] + [list(d) for d in dims])


def group_blocks(g):
    d = GROUPS[g][1]
    if d == 1:
        return [(0, 7, 'kv')] + [(0, 8, 'halo')] + [(0, b, 'full') for b in range(9, 16)]
    if d == 4:
        out = []
        for r in range(4):
            out += [(r, 1, 'kv'), (r, 2, 'halo'), (r, 3, 'full')]
        return out
    return [(r, 0, 'g2') for r in range(16)]


def build_program(stop_after=None, debug=()):
    nc = bass.Bass("TRN2", target_bir_lowering=False)

    def din(name, shape):
        return nc.dram_tensor(name, list(shape), F32, kind="ExternalInput").ap()

    def dout(name, shape, dt=F32):
        return nc.dram_tensor(name, list(shape), dt, kind="ExternalOutput").ap()

    xext = din("xext", [2048, D])
    xs_d = din("xs", [16, D])
    state_d = din("state", [120, 1024])
    ck = [din("ck%d" % g, [4, GROUPS[g][0], 512]) for g in range(3)]
    cv = [din("cv%d" % g, [4, GROUPS[g][0], 512]) for g in range(3)]
    w_in = din("w_in", [D, 10752])
    w_pw = din("w_pw", [1024, D])
    w_o = din("w_o", [512, D])
    w_out = din("w_out", [D, D])
    w_gate = din("w_gate", [D, HID])
    w_up = din("w_up", [D, HID])
    w_down = din("w_down", [HID, D])
    vecs_d = din("vecs", [384, 128])
    gb_d = din("gb", [3, D])
    tabs_d = din("tabs", [3, 128, 16, 64])
    tabS_d = din("tabS", [128, 64])
    masks_d = din("masks", [128, 4, 128])
    ident_d = din("ident", [128, 128])
    smask_d = din("smask", [128, 3, 16])
    sel_d = din("sel", [128, 4, 128])
    ind_d = din("ind", [128, 4])
    hv_d = din("hv", [128, 1])

    y_d = dout("y", [1024, D])
    ys_d = dout("ys", [16, D])
    convp_d = dout("convp", [30, 1024])
    convs_d = dout("convs", [4, 30, 1024])
    own_rows = (128, 512, 1024)
    kp = [dout("kp%d" % g, [own_rows[g], 512]) for g in range(3)]
    vp = [dout("vp%d" % g, [own_rows[g], 512]) for g in range(3)]
    kso = [dout("kso%d" % g, [4, GROUPS[g][0], 512]) for g in range(3)]
    vso = [dout("vso%d" % g, [4, GROUPS[g][0], 512]) for g in range(3)]

    with contextlib.ExitStack() as st:
        S = Sched(nc, st, ring=16)
        ARENA_WORDS = 51456
        arena = st.enter_context(nc.sbuf_tensor("arena", [128, ARENA_WORDS], F32))
        PS = [st.enter_context(nc.psum_tensor("ps%d" % i, [128, 512], F32)) for i in range(8)]
        PSK = [('ps', i) for i in range(8)]

        def psb(i):
            return PS[i][:, :].bitcast(BF16)

        def af(off, n):
            assert off + n <= ARENA_WORDS, (off, n)
            return arena[:, off:off + n]

        def ab(off, nw):
            assert off + nw <= ARENA_WORDS, (off, nw)
            return arena[:, off:off + nw].bitcast(BF16)

        dbg_outs = {}

        def dbg(name, ap, key):
            if name not in debug:
                return
            shape = list(ap.shape)
            dt = ap.dtype
            o = dout("dbg_" + name, shape, dt)
            S.dma('sp', o, ap, reads=(list(key) if isinstance(key, (list, tuple)) and not isinstance(key[0], str) or isinstance(key, list) else [key]), writes=['dbg_' + name])
            dbg_outs[name] = o

        o = 0
        identf = af(o, 128); o += 128
        identb = ab(o, 64); o += 64
        onesf = af(o, 128); o += 128
        onesb = ab(o, 64); o += 64
        mb = v3(ab(o, 256), 4); o += 256
        selb = v3(ab(o, 256), 4); o += 256
        vecT = af(o, 384); o += 384
        ind = af(o, 4); o += 4
        smask = v3(af(o, 48), 3); o += 48
        hv = af(o, 1); o += 4
        tabS = af(o, 64); o += 64
        ssq = af(o, 1); o += 2
        rs = af(o, 1); o += 2
        rstd = af(o, 1); o += 2
        eps_norm = af(o, 1); o += 2
        eps_ln = af(o, 1); o += 2
        o = (o + 63) // 64 * 64
        gbc = af(o, 2048); o += 2048
        stage = af(o, 512); o += 512
        W = []
        for i in range(3):
            W.append(ab(o, 4096)); o += 4096
        OB = o
        WK = ['W0', 'W1', 'W2']

        def mxk(c0, ncols):
            return ['mixT%d' % t for t in range(c0 // 128, min(8, (c0 + ncols - 1) // 128) + 1)]
        MK_ALL = ['mixT%d' % t for t in range(9)]

        def vcol(row):
            return vecT[:, row:row + 1]
        R_BGLU, R_BDW, R_LNG, R_LNB, R_BPW, R_WDW = 16, 32, 40, 48, 56, 128

        S.dma('sp', identf, ident_d, writes=['identf'])
        S.op('dve', lambda e: e.tensor_copy(out=identb, in_=identf), ['identf'], ['identb'])
        S.op('dve', lambda e: e.memset(onesf, 1.0), [], ['onesf'])
        S.op('dve', lambda e: e.memset(onesb, 1.0), [], ['onesb'])
        st4 = v3(stage, 4)
        S.dma('sp', st4, masks_d, writes=['stage'])
        S.op('dve', lambda e: e.tensor_copy(out=mb, in_=st4), ['stage'], ['mb'])
        S.dma('sp', st4, sel_d, reads=[], writes=['stage'])
        S.op('dve', lambda e: e.tensor_copy(out=selb, in_=st4), ['stage'], ['selb'])
        S.dma('sp', ind, ind_d, writes=['ind'])
        S.dma('sp', smask, smask_d, writes=['smask'])
        S.dma('sp', hv, hv_d, writes=['hv'])
        S.dma('sp', tabS, tabS_d, writes=['tabS'])
        vraw = v3(gbc[:, 0:384], 3)
        S.dma('sp', vraw, vecs_d.rearrange("(a p) c -> p a c", p=128), writes=['gbc'])
        for a in range(3):
            S.op('pe', lambda e, a=a: e.transpose(out=PS[0][:, a * 128:(a + 1) * 128], in_=vraw[:, a, :], identity=identf),
                 ['gbc', 'identf'], [PSK[0]])
        S.op('act', lambda e: e.copy(out=vecT, in_=PS[0][:, 0:384]), [PSK[0]], ['vecT'])
        copy_pieces = []
        for g in (2, 1, 0):
            wb = GROUPS[g][0]
            for (dst_, src_, key_) in ((kso[g], ck[g], 'kso%d' % g), (vso[g], cv[g], 'vso%d' % g)):
                if g == 0:
                    copy_pieces.append((dst_[:, 0:wb - 4, :], src_[:, 4:wb, :], key_))
                else:
                    for s_ in range(4):
                        for r0 in range(0, wb - 4, 512):
                            r1 = min(r0 + 512, wb - 4)
                            copy_pieces.append((dst_[s_, r0:r1, :], src_[s_, r0 + 4:r1 + 4, :], key_))

        def issue_copy_piece():
            if copy_pieces:
                d_, s_, k_ = copy_pieces.pop(0)
                S.dma('sp', d_, s_, writes=['%s_piece%d' % (k_, len(copy_pieces))])
        S.dma('sp', convs_d[:, 0:26, :], state_d.rearrange("(s r) c -> s r c", r=30)[:, 4:30, :], writes=['convs'])

        def load_gbc(row):
            S.dma('sp', gbc, gb_d[row:row + 1, :].partition_broadcast(128), writes=['gbc'])

        def wload(wi, src2d, kc, ncols):
            dst = W[wi][:, 0:kc * ncols].rearrange("p (k n) -> p k n", k=kc)
            S.dma('pool', dst, src2d.rearrange("(k p) n -> p k n", p=128), writes=[WK[wi], WK[wi] + 'h0', WK[wi] + 'h1'])
            return dst

        def norm_tile(src, src_key, hb, hb_key, np_=128):
            src, hb = src[0:np_, :], hb[0:np_, :]
            S.op('act', lambda e: e.activation(out=hb, in_=src, func=AF.Square, accum_out=ssq[0:np_, :]), [src_key], [hb_key, 'ssq'])
            S.op('act', lambda e: e.activation(out=rs[0:np_, :], in_=ssq[0:np_, :], func=AF.Ln, scale=1.0 / D, bias=eps_norm[0:np_, :]), ['ssq', 'epsc'], ['rs'])
            S.op('act', lambda e: e.activation(out=rstd[0:np_, :], in_=rs[0:np_, :], func=AF.Exp, scale=-0.5), ['rs'], ['rstd'])
            S.op('dve', lambda e: e.scalar_tensor_tensor(out=hb, in0=src, scalar=rstd[0:np_, :], in1=gbc[0:np_, :], op0=ALU.mult, op1=ALU.mult),
                 [src_key, 'rstd', 'gbc'], [hb_key])

        tctr = [0]

        def transposes(hb, hb_key, dst_fn, dst_key, banks=(2, 3), np_=128, p0=0):
            for half in range(2):
                bi = banks[half]
                bank = psb(bi)
                for c in range(8):
                    cc = 8 * half + c
                    S.op('pe', lambda e: e.transpose(out=bank[:, c * 128:c * 128 + np_], in_=hb[p0:p0 + np_, cc * 128:(cc + 1) * 128], identity=identb[p0:p0 + np_, p0:p0 + np_]),
                         [hb_key, 'identb'], [PSK[bi]])
                eng = 'act' if (tctr[0] % 2 == 0) else 'dve'
                tctr[0] += 1
                dst = dst_fn(8 * half)
                src_ = v3(bank, 8)[:, :, 0:np_]
                if eng == 'act':
                    S.op('act', lambda e: e.copy(out=dst, in_=src_), [PSK[bi]], [dst_key])
                else:
                    S.op('dve', lambda e: e.tensor_copy(out=dst, in_=src_), [PSK[bi]], [dst_key])

        X_OT = OB + 32784
        oT = v3(ab(X_OT, 2080), 4)

        S.op('dve', lambda e: e.memset(eps_norm, NORM_EPS), [], ['epsc'])
        S.op('dve', lambda e: e.memset(eps_ln, LN_EPS), [], ['epsl'])

        o = OB
        ks_all = v3(af(o, 1536), 3); o += 1536
        vs_all = v3(af(o, 1536), 3); o += 1536
        qs_all = v3(af(o, 1536), 3); o += 1536
        A2_BASE = o
        xsb = af(o, 2048); o += 2048
        xblk = [af(o, 2048), af(o + 2048, 2048)]; o += 4096
        hblk = ab(o, 1024); o += 1024
        hTb = [v3(ab(o, 1024), 16), v3(ab(o + 1024, 1024), 16)]; o += 2048
        kst = [af(o, 512), af(o + 512, 512)]; o += 1024
        vst = [af(o, 512), af(o + 512, 512)]; o += 1024
        qst = af(o, 512); o += 512
        ra = v3(af(o, 128), 4); o += 128
        rb = v3(af(o, 128), 4); o += 128
        kb = ab(o, 256); o += 256
        qb = ab(o, 256); o += 256
        kT = [v3(ab(o, 256), 4), v3(ab(o + 256, 256), 4)]; o += 512
        vb = [ab(o, 256), ab(o + 256, 256)]; o += 512
        qT = v3(ab(o, 256), 4); o += 256
        pT = [ab(o, 256), ab(o + 256, 256)]; o += 512
        num = v3(af(o, 4096), 4); o += 4096
        den = v3(af(o, 4096), 4); o += 4096
        tab = v3(af(o, 1024), 16); o += 1024
        X_TAB2 = o; o += 1024
        assert o <= X_OT, o

        def load_xsb(dst, key):
            S.op('dve', lambda e: e.memset(dst, 0.0), [], [key])
            for s in range(4):
                S.dma('sp', dst[32 * s:32 * s + 4, :], xs_d[4 * s:4 * s + 4, :], writes=[key])

        load_xsb(xsb, 'xsb')
        load_gbc(0)

        blkctr = [0]

        def rope(ps_i, tb, out_st, out_key, tkey='tab'):
            ps3 = v3(PS[ps_i][:, :], 4)
            st3 = v3(out_st, 4)
            cc = mkap(tb[:, 0:32], [[0, 4], [1, 32]])
            ms = mkap(tb[:, 32:48], [[0, 4], [1, 16]])
            pp = mkap(tb[:, 48:64], [[0, 4], [1, 16]])
            S.op('act', lambda e: e.copy(out=out_st, in_=PS[ps_i][:, :]), [PSK[ps_i]], [out_key])
            if DEBUG_NOROPE[0]:
                return
            S.op('dve', lambda e: e.tensor_tensor(out=ra, in0=st3[:, :, 0:32], in1=cc, op=ALU.mult), [out_key, tkey], ['ra'])
            S.op('dve', lambda e: e.tensor_tensor(out=rb[:, :, 0:16], in0=st3[:, :, 16:32], in1=ms, op=ALU.mult), [out_key, tkey], ['rb'])
            S.op('dve', lambda e: e.tensor_tensor(out=rb[:, :, 16:32], in0=st3[:, :, 0:16], in1=pp, op=ALU.mult), [out_key, tkey], ['rb'])
            S.op('dve', lambda e: e.tensor_tensor(out=st3[:, :, 0:32], in0=ra, in1=rb, op=ALU.add), ['ra', 'rb'], [out_key])

        def proj(ps_i, hT_, hT_key, wi):
            wv = v3(W[wi][:, :], 16)
            for k in range(KC):
                S.op('pe', lambda e, k=k: e.matmul(PS[ps_i][:, :], lhsT=hT_[:, k, :], rhs=wv[:, k, :], start=(k == 0), stop=(k == KC - 1)),
                     [hT_key, WK[wi]], [PSK[ps_i]])

        def to_T(src_bf, src_key, dst, dst_key, bank):
            bk = psb(bank)
            for h in range(4):
                S.op('pe', lambda e, h=h: e.transpose(out=bk[:, h * 128:(h + 1) * 128], in_=src_bf[:, h * 128:(h + 1) * 128], identity=identb),
                     [src_key, 'identb'], [PSK[bank]])
            S.op('act', lambda e: e.copy(out=dst, in_=v3(bk[:, 0:512], 4)), [PSK[bank]], [dst_key])

        allblocks = []
        for g in range(3):
            for bi_, (r, b, kind) in enumerate(group_blocks(g) + [(0, 0, 'sample')]):
                allblocks.append((g, bi_, r, b, kind))
        loadidx = {}
        li = 0
        for idx, (g, bi_, r, b, kind) in enumerate(allblocks):
            if kind != 'sample':
                loadidx[idx] = li
                li += 1

        def issue_xload(idx):
            g_, bi__, r_, b_, kind_ = allblocks[idx]
            d_ = GROUPS[g_][1]
            xi_ = loadidx[idx] % 2
            e0 = r_ + d_ * 128 * b_
            S.dma('sp', xblk[xi_], xext[e0:e0 + d_ * 127 + 1:d_, :], writes=['xblk%d' % xi_])

        def next_load(idx):
            for j in range(idx + 1, len(allblocks)):
                if allblocks[j][4] != 'sample':
                    return j
            return None

        NB = len(allblocks)
        tab2 = [tab, v3(af(X_TAB2, 1024), 16)]

        def blk_src(idx):
            g, bi_, r, b, kind = allblocks[idx]
            if kind == 'sample':
                return xsb, 'xsb'
            lx = loadidx[idx] % 2
            return xblk[lx], 'xblk%d' % lx

        def do_norm(idx):
            g, bi_, r, b, kind = allblocks[idx]
            src, skey = blk_src(idx)
            norm_tile(src, skey, hblk, 'hblk')
            if kind != 'sample':
                nl = next_load(idx)
                if nl is not None:
                    issue_xload(nl)
            if bi_ >= 4:
                issue_copy_piece()
                issue_copy_piece()

        def do_T(idx):
            xi = idx % 2
            transposes(hblk, 'hblk', lambda c0: hTb[xi][:, c0:c0 + 8, :], 'hTb%d' % xi)

        def do_block(idx, cur):
            g, bi_, r, b, kind = allblocks[idx]
            wb, d = GROUPS[g]
            xi = idx % 2
            hkey = 'hTb%d' % xi
            kk = idx % 2
            tkey = 'tabS' if kind == 'sample' else 'tab%d' % (g % 2)
            tb = tabS if kind == 'sample' else tab2[g % 2][:, bi_, :]
            nxt = idx + 1 if idx + 1 < NB else None
            if nxt is not None:
                g2_, bi2_ = allblocks[nxt][0], allblocks[nxt][1]
                if bi2_ == 0:
                    S.dma('sp', tab2[g2_ % 2], tabs_d[g2_], writes=['tab%d' % (g2_ % 2)])
                do_norm(nxt)
            has_q = kind != 'kv'
            if has_q:
                proj(7, hTb[xi], hkey, 2)
            proj(0, hTb[xi], hkey, 0)
            proj(1, hTb[xi], hkey, 1)
            if kind == 'sample':
                k_st, k_key = ks_all[:, g, :], 'ks_all'
                v_st, v_key = vs_all[:, g, :], 'vs_all'
            else:
                k_st, k_key = kst[kk], 'kst%d' % kk
                v_st, v_key = vst[kk], 'vst%d' % kk
            if kind == 'sample':
                rope(7, tb, qs_all[:, g, :], 'qs_all', tkey)
                rope(0, tb, k_st, k_key, tkey)
                S.op('act', lambda e: e.copy(out=v_st, in_=PS[1][:, :]), [PSK[1]], [v_key])
                for s_ in range(4):
                    S.dma('sp', kso[g][s_, wb - 4:wb, :], ks_all[32 * s_:32 * s_ + 4, g, :], reads=['ks_all'], writes=['kso%d_new%d' % (g, s_)])
                    S.dma('sp', vso[g][s_, wb - 4:wb, :], vs_all[32 * s_:32 * s_ + 4, g, :], reads=['vs_all'], writes=['vso%d_new%d' % (g, s_)])
                if nxt is not None:
                    do_T(nxt)
                return
            if has_q:
                rope(7, tb, qst, 'qst', tkey)
                S.op('dve', lambda e: e.tensor_copy(out=qb, in_=qst), ['qst'], ['qb'])
            rope(0, tb, k_st, k_key, tkey)
            S.op('dve', lambda e: e.tensor_copy(out=kb, in_=k_st), [k_key], ['kb'])
            S.op('act', lambda e: e.copy(out=v_st, in_=PS[1][:, :]), [PSK[1]], [v_key])
            S.op('dve', lambda e: e.tensor_copy(out=vb[cur], in_=v_st), [v_key], ['vb%d' % cur])
            if has_q:
                to_T(qb, 'qb', qT, 'qT', 5)
            to_T(kb, 'kb', kT[cur], 'kT%d' % cur, 6)
            if g == 0 and b == 15:
                S.dma('sp', kp[0], k_st, reads=[k_key], writes=['kp0'])
                S.dma('sp', vp[0], v_st, reads=[v_key], writes=['vp0'])
            if g == 1 and b == 3:
                S.dma('sp', kp[1][r:r + 4 * 127 + 1:4, :], k_st, reads=[k_key], writes=['kp1'])
                S.dma('sp', vp[1][r:r + 4 * 127 + 1:4, :], v_st, reads=[v_key], writes=['vp1'])
            if g == 2:
                S.dma('sp', kp[2][r:r + 16 * 63 + 1:16, :], k_st[64:128, :], reads=[k_key], writes=['kp2'])
                S.dma('sp', vp[2][r:r + 16 * 63 + 1:16, :], v_st[64:128, :], reads=[v_key], writes=['vp2'])
            if nxt is not None:
                do_T(nxt)
            if not has_q:
                return
            if kind == 'g2':
                keyblocks = [(cur, 3)]
                q0, nq = 64, 64
            elif kind == 'halo':
                keyblocks = [(cur ^ 1, 2), (cur, 0)]
                q0, nq = 0, 128
            else:
                keyblocks = [(cur ^ 1, 1), (cur, 0)]
                q0, nq = 0, 128
            sbanks = (4, 5)
            for ki, (kbuf, mi) in enumerate(keyblocks):
                sb_ = sbanks[ki]
                for h in range(4):
                    S.op('pe', lambda e: e.matmul(PS[sb_][:, h * nq:(h + 1) * nq], lhsT=kT[kbuf][:, h, :], rhs=qT[:, h, q0:q0 + nq], start=True, stop=False),
                         ['kT%d' % kbuf, 'qT'], [PSK[sb_]])
                    S.op('pe', lambda e: e.matmul(PS[sb_][:, h * nq:(h + 1) * nq], lhsT=identb, rhs=mb[:, mi, q0:q0 + nq], start=False, stop=True),
                         ['identb', 'mb'], [PSK[sb_]])
                S.op('act', lambda e: e.activation(out=pT[ki][:, 0:4 * nq], in_=PS[sb_][:, 0:4 * nq], func=AF.Exp, scale=SCALE),
                     [PSK[sb_]], ['pT%d' % ki])
            nkb = len(keyblocks)
            OBK, DBK = 0, 1
            for h in range(4):
                for ki, (kbuf, mi) in enumerate(keyblocks):
                    S.op('pe', lambda e: e.matmul(PS[OBK][:, h * nq:(h + 1) * nq], lhsT=vb[kbuf][:, h * 128:(h + 1) * 128], rhs=pT[ki][:, h * nq:(h + 1) * nq], start=(ki == 0), stop=(ki == nkb - 1)),
                         ['vb%d' % kbuf, 'pT%d' % ki], [PSK[OBK]])
            for ki in range(nkb):
                S.op('pe', lambda e: e.matmul(PS[DBK][:, 0:4 * nq], lhsT=onesb, rhs=pT[ki][:, 0:4 * nq], start=(ki == 0), stop=(ki == nkb - 1)),
                     ['onesb', 'pT%d' % ki], [PSK[DBK]])
            if g == 0:
                cols = slice(128 * (b - 8), 128 * (b - 8) + 128)
            elif g == 1:
                c0 = r + 512 * (b - 2)
                cols = slice(c0, c0 + 4 * 127 + 1, 4)
            else:
                cols = slice(r, r + 16 * 63 + 1, 16)
            po = v3(PS[OBK][:, 0:4 * nq], 4)
            pd = v3(PS[DBK][:, 0:4 * nq], 4)
            nsl = num[:, :, cols]
            dsl = den[:, :, cols]
            if g == 0:
                S.op('act', lambda e: e.copy(out=nsl, in_=po), [PSK[OBK]], ['num'])
                S.op('dve', lambda e: e.tensor_copy(out=dsl, in_=pd), [PSK[DBK]], ['den'])
            else:
                S.op('dve', lambda e: e.tensor_tensor(out=nsl, in0=po, in1=nsl, op=ALU.add), [PSK[OBK], 'num'], ['num'])
                S.op('dve', lambda e: e.tensor_tensor(out=dsl, in0=pd, in1=dsl, op=ALU.add), [PSK[DBK], 'den'], ['den'])

        if stop_after != 'setup':
            issue_xload(0)
            S.dma('sp', tab2[0], tabs_d[0], writes=['tab0'])
            do_norm(0)
            do_T(0)
            cur = 0
            for idx, (g, bi_, r, b, kind) in enumerate(allblocks):
                if bi_ == 0:
                    wload(2, w_in[:, O1 + 512 * g:O1 + 512 * (g + 1)], 16, 512)
                    wload(0, w_in[:, O2 + 512 * g:O2 + 512 * (g + 1)], 16, 512)
                    wload(1, w_in[:, O3 + 512 * g:O3 + 512 * (g + 1)], 16, 512)
                cur ^= 1
                do_block(idx, cur)

        while copy_pieces:
            issue_copy_piece()
        if stop_after not in ('setup',):
            S.op('act', lambda e: e.activation(out=den[:, :, :], in_=den[:, :, :], func=AF.Ln), ['den'], ['den'])
            S.op('act', lambda e: e.activation(out=den[:, :, :], in_=den[:, :, :], func=AF.Exp, scale=-1.0), ['den'], ['den'])
            S.op('dve', lambda e: e.tensor_tensor(out=oT[:, :, 0:1024], in0=num[:, :, :], in1=den[:, :, :], op=ALU.mult), ['num', 'den'], ['oT'])
        dbg('ks_all', ks_all[:, :, :], 'ks_all')
        dbg('qs_all', qs_all[:, :, :], 'qs_all')
        dbg('oT1', oT[:, :, :], 'oT')

        if stop_after in ('setup', 'A1'):
            S.barrier(('sp',))
            S.emit()
            return nc, dbg_outs

        S.barrier()
        o = A2_BASE
        qs_bf = v3(ab(o, 768), 3); o += 768
        qbn2 = [v3(af(o, 2048), 4), v3(af(o + 2048, 2048), 4)]; o += 4096
        prod2 = [af(o, 2048), af(o + 2048, 2048)]; o += 4096
        prod = prod2[0]
        prodV = ab(o, 1024); o += 1024
        indb = ab(o, 2); o += 2
        Kc2 = [v3(af(o, 2048), 4), v3(af(o + 2048, 2048), 4)]; o += 4096
        Vc2 = [v3(af(o, 2048), 4), v3(af(o + 2048, 2048), 4)]; o += 4096
        Pn_all = v3(af(o, 48), 3); o += 48
        Sc = af(o, 16); o += 16
        Pc = af(o, 16); o += 16
        rd = af(o, 16); o += 16
        o = (o + 63) // 64 * 64
        pvn_all = v3(ab(o, 3072), 3); o += 3072
        osm = af(o, 2048); o += 2048
        osn = ab(o, 1024); o += 1024
        assert o <= X_OT, o

        S.op('dve', lambda e: e.tensor_copy(out=qs_bf, in_=qs_all[:, :, :]), ['qs_all'], ['qs_bf'])
        S.op('dve', lambda e: e.tensor_copy(out=indb, in_=ind), ['ind'], ['indb'])

        def bc_i(ap2d):
            return mkap(ap2d, [[0, 4], [1, 512]])

        def as_ihd(ap, istep):
            return mkap(ap, [[istep, 4], [128, 4], [1, 128]])

        def p_bc(ap16):
            return mkap(ap16, [[4, 4], [1, 4], [0, 128]])

        for g in range(3):
            qbn = qbn2[g % 2]
            qk = 'qbn%d' % (g % 2)
            for i in range(4):
                bk = i % 2
                S.op('pe', lambda e: e.matmul(PS[bk][:, :], lhsT=selb[:, i, :], rhs=qs_bf[:, g, :], start=True, stop=True),
                     ['selb', 'qs_bf'], [PSK[bk]])
                S.op('act', lambda e: e.copy(out=qbn[:, i, :], in_=PS[bk][:, :]), [PSK[bk]], [qk])
            prod3 = v3(prod, 4)
            S.op('dve', lambda e: e.tensor_tensor(out=prod3, in0=bc_i(ks_all[:, g, :]), in1=qbn[:, :, :], op=ALU.mult), ['ks_all', qk], ['prod'])
            S.op('dve', lambda e: e.tensor_reduce(out=Sc, in_=v3(prod, 16), axis=AX.X, op=ALU.add), ['prod'], ['Sc'])
            S.op('act', lambda e: e.activation(out=Pn_all[:, g, :], in_=Sc, func=AF.Exp, scale=SCALE), ['Sc'], ['Pn_all'])
            mrow = 1 if g == 0 else 2
            S.op('dve', lambda e: e.tensor_tensor(out=Pn_all[:, g, :], in0=Pn_all[:, g, :], in1=smask[:, mrow, :], op=ALU.mult), ['Pn_all', 'smask'], ['Pn_all'])
            S.op('dve', lambda e: e.tensor_tensor(out=as_ihd(pvn_all[:, g, :], 512), in0=as_ihd(vs_all[:, g, :], 0), in1=p_bc(Pn_all[:, g, :]), op=ALU.mult),
                 ['vs_all', 'Pn_all'], ['pvn_all'])

        iters = [(s_, g) for s_ in range(4) for g in range(3)]

        def stage_x(it):
            s_, g = iters[it]
            wb, d = GROUPS[g]
            bi = it % 2
            Kc, Vc, qbn = Kc2[bi], Vc2[bi], qbn2[bi]
            kkey, vkey, qk = 'Kc%d' % bi, 'Vc%d' % bi, 'qbn%d' % bi
            if g == 0:
                S.dma('sp', Kc[:, 0, :], ck[0][s_], writes=[kkey])
                S.dma('sp', Vc[:, 0, :], cv[0][s_], writes=[vkey])
            else:
                S.dma('sp', Kc[:, :, :], ck[g][s_].rearrange("(p q) c -> p q c", q=d)[:, 0:4, :], writes=[kkey])
                S.dma('sp', Vc[:, :, :], cv[g][s_].rearrange("(p q) c -> p q c", q=d)[:, 0:4, :], writes=[vkey])
            for i in range(4):
                bk = i % 2
                col = identb[:, 32 * s_ + i:32 * s_ + i + 1]
                selc = mkap(col, [[0, 128]])
                S.op('pe', lambda e: e.matmul(PS[bk][:, :], lhsT=selc, rhs=qs_bf[:, g, :], start=True, stop=True),
                     ['identb', 'qs_bf'], [PSK[bk]])
                S.op('act', lambda e: e.copy(out=qbn[:, i, :], in_=PS[bk][:, :]), [PSK[bk]], [qk])

        def stage_y(it):
            s_, g = iters[it]
            bi = it % 2
            Kc, Vc, qbn = Kc2[bi], Vc2[bi], qbn2[bi]
            kkey, vkey, qk = 'Kc%d' % bi, 'Vc%d' % bi, 'qbn%d' % bi
            if g == 0:
                kview, vview = bc_i(Kc[:, 0, :]), as_ihd(Vc[:, 0, :], 0)
            else:
                kview, vview = Kc[:, :, :], as_ihd(Vc[:, 0, :], 512)
            prod_ = prod2[bi]
            prod3 = v3(prod_, 4)
            S.op(A2_PROD_ENG[0], lambda e: e.tensor_tensor(out=prod3, in0=kview, in1=qbn[:, :, :], op=ALU.mult), [kkey, qk], ['prod%d' % bi])
            S.op('dve', lambda e: e.tensor_reduce(out=Sc, in_=v3(prod_, 16), axis=AX.X, op=ALU.add), ['prod%d' % bi], ['Sc'])
            S.op('act', lambda e: e.activation(out=Pc, in_=Sc, func=AF.Exp, scale=SCALE), ['Sc'], ['Pc'])
            if g == 0:
                S.op('dve', lambda e: e.tensor_tensor(out=Pc, in0=Pc, in1=smask[:, 0, :], op=ALU.mult), ['Pc', 'smask'], ['Pc'])
            S.op('dve', lambda e: e.tensor_tensor(out=as_ihd(prodV, 512), in0=vview, in1=p_bc(Pc), op=ALU.mult), [vkey, 'Pc'], ['prodV'])
            for j in range(4):
                S.op('pe', lambda e: e.matmul(PS[2 + j][0:1, :], lhsT=onesb[:, 0:1], rhs=prodV[:, 512 * j:512 * (j + 1)], start=(g == 0), stop=False),
                     ['onesb', 'prodV'], [PSK[2 + j]])
                S.op('pe', lambda e: e.matmul(PS[2 + j][0:1, :], lhsT=indb[:, s_:s_ + 1], rhs=pvn_all[:, g, 512 * j:512 * (j + 1)], start=False, stop=(g == 2)),
                     ['indb', 'pvn_all'], [PSK[2 + j]])
            S.op('pe', lambda e: e.matmul(PS[6][0:1, 0:16], lhsT=onesf[:, 0:1], rhs=Pc, start=(g == 0), stop=False), ['onesf', 'Pc'], [PSK[6]])
            S.op('pe', lambda e: e.matmul(PS[6][0:1, 0:16], lhsT=ind[:, s_:s_ + 1], rhs=Pn_all[:, g, :], start=False, stop=(g == 2)), ['ind', 'Pn_all'], [PSK[6]])
            if g != 2:
                return
            for j in range(4):
                S.op('act', lambda e: e.copy(out=osm[0:1, 512 * j:512 * (j + 1)], in_=PS[2 + j][0:1, :]), [PSK[2 + j]], ['osm'])
            S.op('dve', lambda e: e.reciprocal(out=rd[0:1, :], in_=PS[6][0:1, 0:16]), [PSK[6]], ['rd'])
            S.op('dve', lambda e: e.tensor_tensor(out=v3(osn[0:1, :], 16), in0=v3(osm[0:1, :], 16), in1=mkap(rd[0:1, :], [[1, 16], [0, 128]]), op=ALU.mult),
                 ['osm', 'rd'], ['osn'])
            for c in range(16):
                S.op('pe', lambda e: e.matmul(PS[7][:, c:c + 1], lhsT=osn[0:1, c * 128:(c + 1) * 128], rhs=onesb[0:1, 0:1], start=True, stop=True),
                     ['osn', 'onesb'], [PSK[7]])
            dstv = oT[:, :, 1024 + 4 * s_:1024 + 4 * s_ + 4].rearrange("p h i -> p i h")
            S.op('act', lambda e: e.copy(out=dstv, in_=PS[7][:, 0:16].rearrange("p (i h) -> p i h", i=4)), [PSK[7]], ['oT'])

        stage_x(0)
        for it in range(len(iters)):
            if it + 1 < len(iters):
                stage_x(it + 1)
            stage_y(it)
        dbg('oT2', oT[:, :, :], 'oT')
        if stop_after == 'A2':
            S.barrier(('sp',))
            S.emit()
            return nc, dbg_outs

        S.barrier()
        yT = v3(af(OB, 8320), 8)
        hT = v3(ab(OB + 9216, 8560), 16)
        cT = v3(ab(OB + 19456, 4160), 8)
        o = OB + 19456
        xsb3 = af(o, 2048); o += 2048
        xb3 = [af(o, 2048), af(o + 2048, 2048)]; o += 4096
        hblk3 = ab(o, 1024); o += 1024
        assert o == OB + 26624
        LN_BASE = OB + 26624
        u2 = af(o, 384); o += 384
        sg = af(o, 512); o += 512
        stT = v3(af(o, 960), 8); o += 960
        o = OB + 30736
        cpo2 = af(o, 1024); o += 1024
        cso2 = af(o, 1024); o += 1024
        assert o <= X_OT

        S.dma('sp', xsb3[0:16, :], xs_d, writes=['xsb3'])
        straw = xb3[1][0:120, 0:1024]
        S.dma('sp', straw, state_d, writes=['xb3_1'])
        for c in range(8):
            bk = 4 + (c // 4)
            S.op('pe', lambda e, c=c, bk=bk: e.transpose(out=PS[bk][:, (c % 4) * 120:(c % 4) * 120 + 120], in_=straw[:, c * 128:(c + 1) * 128], identity=identf[0:120, 0:120]),
                 ['xb3_1', 'identf'], [PSK[bk]])
        for hh in range(2):
            S.op('act', lambda e, hh=hh: e.copy(out=stT[:, 4 * hh:4 * hh + 4, :], in_=v3(PS[4 + hh][:, 0:480], 4)), [PSK[4 + hh]], ['stT'])
        def glu_views(hh):
            wa_ = W[0][:, 4096 * hh:4096 * (hh + 1)].rearrange("p (k n) -> p k n", k=16)
            wb_ = W[1][:, 4096 * hh:4096 * (hh + 1)].rearrange("p (k n) -> p k n", k=16)
            return wa_, wb_

        def glu_load(ph):
            hh = ph % 2
            wa_, wb_ = glu_views(hh)
            S.dma('pool', wa_, w_in[:, 256 * ph:256 * (ph + 1)].rearrange("(k p) n -> p k n", p=128), writes=['W0h%d' % hh])
            S.dma('pool', wb_, w_in[:, 1024 + 256 * ph:1024 + 256 * (ph + 1)].rearrange("(k p) n -> p k n", p=128), writes=['W1h%d' % hh])

        glu_load(0)
        glu_load(1)
        hb3 = [hblk3, ab(LN_BASE + 1856, 1024)]

        def a3_src(t):
            if t == 8:
                return xsb3, 'xsb3'
            xi = 1 if t == 9 else t % 2
            return xb3[xi], 'xb3_%d' % xi

        A3NP = [128] * 8 + [16, 30]
        A3C0 = [128 * t for t in range(8)] + [1024, 1040]

        def a3_ld(t):
            if t == 8:
                return
            dst, key = a3_src(t)
            if t == 9:
                S.dma('sp', dst[0:30, :], xext[994:1024, :], writes=[key])
            else:
                S.dma('sp', dst, xext[1024 + 128 * t:1024 + 128 * (t + 1), :], writes=[key])

        def a3_norm(t):
            src, skey = a3_src(t)
            norm_tile(src, skey, hb3[t % 2], 'hblk3_%d' % (t % 2), np_=A3NP[t])

        a3_ld(0)
        a3_ld(1)
        a3_norm(0)
        for t in range(10):
            if t + 2 < 10:
                a3_ld(t + 2)
            if t + 1 < 10:
                a3_norm(t + 1)
            transposes(hb3[t % 2], 'hblk3_%d' % (t % 2), lambda c0, t=t: hT[:, c0:c0 + 8, A3C0[t]:A3C0[t] + A3NP[t]], 'hT', np_=A3NP[t])
        dbg('hT', hT[:, :, :], 'hT')

        TG3 = [(0, 357), (357, 357), (714, 356)]
        S.barrier()
        o = OB + 19456
        dg = [v3(ab(o, 1984), 31), v3(ab(o + 1984, 1984), 31)]; o += 3968
        ubb = [ab(o, 528), ab(o + 528, 528)]; o += 1056
        ucb = [v3(ab(o, 68), 4), v3(ab(o + 68, 68), 4)]; o += 136
        assert o <= OB + 26624
        for ph in range(4):
            if ph >= 1 and ph + 1 < 4:
                glu_load(ph + 1)
            gh = ph % 2
            wa, wbb = glu_views(gh)
            for n4 in range(2):
                n = 2 * ph + n4
                ui = n % 2
                ukey = 'ubb%d' % ui
                uckey = 'ucb%d' % ui
                dkey_ = 'dg%d' % ui
                S.op('dve', lambda e: e.tensor_tensor(out=dg[ui][:, :, :], in0=mkap(identb, [[0, 31], [1, 128]]),
                                                      in1=mkap(vecT[:, R_WDW + n:R_WDW + n + 1], [[8, 31], [0, 128]]), op=ALU.mult),
                     ['identb', 'vecT'], [dkey_])
                S.op('act', lambda e: e.copy(out=ucb[ui][:, :, 0:30], in_=stT[:, n, :].rearrange("p (s r) -> p s r", r=30)), ['stT'], [uckey])
                for ti, (c0, nc_) in enumerate(TG3):
                    pa = 0 if (ti % 2 == 0) else 7
                    pb = 1
                    for k in range(KC):
                        S.op('pe', lambda e: e.matmul(PS[pa][:, 0:nc_], lhsT=wa[:, k, n4 * 128:(n4 + 1) * 128], rhs=hT[:, k, c0:c0 + nc_], start=(k == 0), stop=(k == KC - 1)),
                             ['hT', 'W0h%d' % gh], [PSK[pa]])
                    for k in range(KC):
                        S.op('pe', lambda e: e.matmul(PS[pb][:, 0:nc_], lhsT=wbb[:, k, n4 * 128:(n4 + 1) * 128], rhs=hT[:, k, c0:c0 + nc_], start=(k == 0), stop=(k == KC - 1)),
                             ['hT', 'W1h%d' % gh], [PSK[pb]])
                    S.op('act', lambda e: e.activation(out=sg[:, 0:nc_], in_=PS[pb][:, 0:nc_], func=AF.Sigmoid, bias=vcol(R_BGLU + 8 + n)),
                         [PSK[pb], 'vecT'], ['sg'])
                    if ti < 2:
                        dst, dkey = ubb[ui][:, 30 + c0:30 + c0 + nc_], ukey
                    else:
                        dst, dkey = u2[:, 0:nc_], 'u2'
                    S.op('dve', lambda e: e.scalar_tensor_tensor(out=dst, in0=PS[pa][:, 0:nc_], scalar=vcol(R_BGLU + n), in1=sg[:, 0:nc_], op0=ALU.add, op1=ALU.mult),
                         [PSK[pa], 'sg', 'vecT'], [dkey])
                S.op('act', lambda e: e.copy(out=ubb[ui][:, 30 + 714:30 + 1024], in_=u2[:, 0:310]), ['u2'], [ukey])
                S.op('act', lambda e: e.copy(out=ucb[ui][:, :, 30:34], in_=u2[:, 310:326].rearrange("p (s j) -> p s j", j=4)), ['u2'], [uckey])
                S.op('dve', lambda e: e.tensor_scalar(out=ubb[ui][:, 0:30], in0=u2[:, 326:356], scalar1=hv, scalar2=None, op0=ALU.mult), ['u2', 'hv'], [ukey])
                S.op('pe', lambda e: e.transpose(out=PS[4][:, 0:128], in_=u2[:, 182:310], identity=identf), ['u2', 'identf'], [PSK[4]])
                S.op('act', lambda e: e.copy(out=cpo2[:, n * 128:(n + 1) * 128], in_=PS[4][:, 0:128]), [PSK[4]], ['cpo2'])
                S.op('pe', lambda e: e.transpose(out=PS[5][0:16, 0:128], in_=u2[:, 310:326], identity=identf), ['u2', 'identf'], [PSK[5]])
                S.op('act', lambda e: e.copy(out=cso2[0:16, n * 128:(n + 1) * 128], in_=PS[5][0:16, 0:128]), [PSK[5]], ['cso2'])
                ysd = yT[:, n, 1024:1040].rearrange("p (s j) -> p s j", j=4)
                for j in range(31):
                    S.op('pe', lambda e: e.matmul(PS[2][:, :], lhsT=dg[ui][:, j, :], rhs=ubb[ui][:, j:j + 512], start=(j == 0), stop=(j == 30)),
                         [dkey_, ukey], [PSK[2]])
                    S.op('pe', lambda e: e.matmul(PS[3][:, :], lhsT=dg[ui][:, j, :], rhs=ubb[ui][:, 512 + j:512 + j + 512], start=(j == 0), stop=(j == 30)),
                         [dkey_, ukey], [PSK[3]])
                    S.op('pe', lambda e: e.matmul(PS[6][:, 0:16].rearrange("p (s j) -> p s j", j=4), lhsT=dg[ui][:, j, :], rhs=ucb[ui][:, :, j:j + 4], start=(j == 0), stop=(j == 30)),
                         [dkey_, uckey], [PSK[6]])
                S.op('act', lambda e: e.activation(out=yT[:, n, 0:512], in_=PS[2][:, :], func=AF.Identity, bias=vcol(R_BDW + n)), [PSK[2], 'vecT'], ['yT'])
                S.op('dve', lambda e: e.tensor_scalar(out=yT[:, n, 512:1024], in0=PS[3][:, :], scalar1=vcol(R_BDW + n), scalar2=None, op0=ALU.add), [PSK[3], 'vecT'], ['yT'])
                S.op('dve', lambda e: e.tensor_scalar(out=ysd, in0=PS[6][:, 0:16].rearrange("p (s j) -> p s j", j=4), scalar1=vcol(R_BDW + n), scalar2=None, op0=ALU.add), [PSK[6], 'vecT'], ['yT'])
        S.dma('sp', convp_d, cpo2[98:128, :], reads=['cpo2'], writes=['convp'])
        for s in range(4):
            S.dma('sp', convs_d[s, 26:30, :], cso2[4 * s:4 * s + 4, :], reads=['cso2'], writes=['convs'])
        dbg('yT', yT[:, :, :], 'yT')

        def a4_views(hh):
            wgc = W[0][:, 4096 * hh:4096 * (hh + 1)].rearrange("p (k n) -> p k n", k=16)
            wga = W[1][:, 4096 * hh:4096 * (hh + 1)].rearrange("p (k n) -> p k n", k=16)
            w2a = W[2][:, 3072 * hh:3072 * hh + 2048].rearrange("p (k n) -> p k n", k=8)
            w2b = W[2][:, 3072 * hh + 2048:3072 * (hh + 1)].rearrange("p (k n) -> p k n", k=4)
            return wgc, wga, w2a, w2b

        def a4_load(jh):
            hh = jh % 2
            wgc, wga, w2a, w2b = a4_views(hh)
            cs = slice(256 * jh, 256 * (jh + 1))
            S.dma('pool', wgc, w_in[:, O4 + 256 * jh:O4 + 256 * (jh + 1)].rearrange("(k p) n -> p k n", p=128), writes=['W0h%d' % hh])
            S.dma('pool', wga, w_in[:, O5 + 256 * jh:O5 + 256 * (jh + 1)].rearrange("(k p) n -> p k n", p=128), writes=['W1h%d' % hh])
            S.dma('pool', w2a, w_pw[:, cs].rearrange("(k p) n -> p k n", p=128), writes=['W2ah%d' % hh])
            S.dma('pool', w2b, w_o[:, cs].rearrange("(k p) n -> p k n", p=128), writes=['W2bh%d' % hh])

        a4_load(0)
        a4_load(1)
        S.barrier()
        o = LN_BASE
        LW = 352
        ysq = [af(o, LW), af(o + LW, LW)]; o += 2 * LW
        mean3 = [af(o + i * LW, LW) for i in range(3)]; o += 3 * LW
        msq = af(o, LW); o += LW
        lrs3 = [af(o + i * LW, LW) for i in range(3)]; o += 3 * LW
        tt = [af(o, LW), af(o + LW, LW)]; o += 2 * LW
        assert o <= OB + 30736
        TGN = [(0, 347), (347, 347), (694, 346)]
        for ti, (c0, nc_) in enumerate(TGN):
            b0, b1 = 2 * ti, 2 * ti + 1
            for n in range(8):
                S.op('pe', lambda e: e.matmul(PS[b0][:, 0:nc_], lhsT=onesf, rhs=yT[:, n, c0:c0 + nc_], start=(n == 0), stop=(n == 7)),
                     ['yT', 'onesf'], [PSK[b0]])
                qi = n % 2
                S.op('act', lambda e: e.activation(out=ysq[qi][:, 0:nc_], in_=yT[:, n, c0:c0 + nc_], func=AF.Square), ['yT'], ['ysq%d' % qi])
                S.op('pe', lambda e: e.matmul(PS[b1][:, 0:nc_], lhsT=onesf, rhs=ysq[qi][:, 0:nc_], start=(n == 0), stop=(n == 7)),
                     ['ysq%d' % qi, 'onesf'], [PSK[b1]])
        for ti, (c0, nc_) in enumerate(TGN):
            b0, b1 = 2 * ti, 2 * ti + 1
            mean, lrs = mean3[ti], lrs3[ti]
            mk, lk = 'mean%d' % ti, 'lrs%d' % ti
            S.op('act', lambda e: e.mul(out=mean[:, 0:nc_], in_=PS[b0][:, 0:nc_], mul=1.0 / 1024), [PSK[b0]], [mk])
            S.op('dve', lambda e: e.tensor_tensor(out=msq[:, 0:nc_], in0=mean[:, 0:nc_], in1=mean[:, 0:nc_], op=ALU.mult), [mk], ['msq'])
            S.op('dve', lambda e: e.scalar_tensor_tensor(out=lrs[:, 0:nc_], in0=PS[b1][:, 0:nc_], scalar=1.0 / 1024, in1=msq[:, 0:nc_], op0=ALU.mult, op1=ALU.subtract),
                 [PSK[b1], 'msq'], [lk])
            S.op('act', lambda e: e.activation(out=lrs[:, 0:nc_], in_=lrs[:, 0:nc_], func=AF.Ln, bias=eps_ln), [lk, 'epsl'], [lk])
            S.op('act', lambda e: e.activation(out=lrs[:, 0:nc_], in_=lrs[:, 0:nc_], func=AF.Exp, scale=-0.5), [lk], [lk])
        for ti, (c0, nc_) in enumerate(TGN):
            mean, lrs = mean3[ti], lrs3[ti]
            mk, lk = 'mean%d' % ti, 'lrs%d' % ti
            for n in range(8):
                qi = n % 2
                S.op('dve', lambda e: e.tensor_tensor(out=tt[qi][:, 0:nc_], in0=yT[:, n, c0:c0 + nc_], in1=mean[:, 0:nc_], op=ALU.subtract),
                     ['yT', mk], ['tt%d' % qi])
                S.op('dve', lambda e: e.tensor_tensor(out=tt[qi][:, 0:nc_], in0=tt[qi][:, 0:nc_], in1=lrs[:, 0:nc_], op=ALU.mult),
                     ['tt%d' % qi, lk], ['tt%d' % qi])
                S.op('act', lambda e: e.activation(out=cT[:, n, c0:c0 + nc_], in_=tt[qi][:, 0:nc_], func=AF.Silu, scale=vcol(R_LNG + n), bias=vcol(R_LNB + n)),
                     ['tt%d' % qi, 'vecT'], ['cT'])
        dbg('cT', cT[:, :, :], 'cT')
        if stop_after == 'A3':
            S.barrier(('sp',))
            S.emit()
            return nc, dbg_outs

        S.barrier()
        mixT = v3(ab(OB, 8320), 16)
        o = OB + 24064
        sgc = [af(o, 512), af(o + 512, 512)]; o += 1024
        sga = [af(o, 512), af(o + 512, 512)]; o += 1024
        t1 = [af(o, 512), af(o + 512, 512)]; o += 1024
        t2 = [af(o, 512), af(o + 512, 512)]; o += 1024
        it = 0

        for jh in range(8):
            if jh >= 1 and jh + 1 < 8:
                a4_load(jh + 1)
            hh = jh % 2
            wgc, wga, w2a, w2b = a4_views(hh)
            for n2 in range(2):
                n = 2 * jh + n2
                for (c0, nc_) in TGN:
                    pb_ = 4 * (it % 2)
                    q = it % 2
                    it += 1
                    wsl = slice(n2 * 128, (n2 + 1) * 128)
                    for k in range(KC):
                        S.op('pe', lambda e: e.matmul(PS[pb_][:, 0:nc_], lhsT=wgc[:, k, wsl], rhs=hT[:, k, c0:c0 + nc_], start=(k == 0), stop=(k == KC - 1)),
                             ['hT', 'W0h%d' % hh], [PSK[pb_]])
                    for k in range(KC):
                        S.op('pe', lambda e: e.matmul(PS[pb_ + 1][:, 0:nc_], lhsT=wga[:, k, wsl], rhs=hT[:, k, c0:c0 + nc_], start=(k == 0), stop=(k == KC - 1)),
                             ['hT', 'W1h%d' % hh], [PSK[pb_ + 1]])
                    for k in range(8):
                        S.op('pe', lambda e: e.matmul(PS[pb_ + 2][:, 0:nc_], lhsT=w2a[:, k, wsl], rhs=cT[:, k, c0:c0 + nc_], start=(k == 0), stop=(k == 7)),
                             ['cT', 'W2ah%d' % hh], [PSK[pb_ + 2]])
                    for k in range(4):
                        S.op('pe', lambda e: e.matmul(PS[pb_ + 3][:, 0:nc_], lhsT=w2b[:, k, wsl], rhs=oT[:, k, c0:c0 + nc_], start=(k == 0), stop=(k == 3)),
                             ['oT', 'W2bh%d' % hh], [PSK[pb_ + 3]])
                    S.op('act', lambda e: e.activation(out=sgc[q][:, 0:nc_], in_=PS[pb_][:, 0:nc_], func=AF.Sigmoid), [PSK[pb_]], ['sgc%d' % q])
                    S.op('act', lambda e: e.activation(out=sga[q][:, 0:nc_], in_=PS[pb_ + 1][:, 0:nc_], func=AF.Sigmoid), [PSK[pb_ + 1]], ['sga%d' % q])
                    S.op('dve', lambda e: e.scalar_tensor_tensor(out=t1[q][:, 0:nc_], in0=PS[pb_ + 2][:, 0:nc_], scalar=vcol(R_BPW + n), in1=sgc[q][:, 0:nc_], op0=ALU.add, op1=ALU.mult),
                         [PSK[pb_ + 2], 'sgc%d' % q, 'vecT'], ['t1%d' % q])
                    S.op('dve', lambda e: e.tensor_tensor(out=t2[q][:, 0:nc_], in0=PS[pb_ + 3][:, 0:nc_], in1=sga[q][:, 0:nc_], op=ALU.mult),
                         [PSK[pb_ + 3], 'sga%d' % q], ['t2%d' % q])
                    S.op('dve', lambda e: e.tensor_tensor(out=mixT[:, n, c0:c0 + nc_], in0=t1[q][:, 0:nc_], in1=t2[q][:, 0:nc_], op=ALU.add),
                         ['t1%d' % q, 't2%d' % q], mxk(c0, nc_))
        dbg('mixT', mixT[:, :, :], list(MK_ALL))
        if stop_after == 'A4':
            S.barrier(('sp',))
            S.emit()
            return nc, dbg_outs

        def a5_buf(jh):
            wi, hh = (jh % 4) // 2, jh % 2
            ws_ = W[wi][:, 4096 * hh:4096 * (hh + 1)].rearrange("p (k n) -> p k n", k=16)
            return ws_, 'W%dh%d' % (wi, hh)

        def a5_load(jh):
            ws_, key_ = a5_buf(jh)
            S.dma('pool', ws_, w_out[:, 256 * jh:256 * (jh + 1)].rearrange("(k p) n -> p k n", p=128), writes=[key_])

        for jh in range(4):
            a5_load(jh)
        S.barrier()
        xacc = v3(af(OB + 9216, 18432), 9)
        o = OB + 27648
        hblk5 = ab(o, 1024); o += 1024
        aq = v3(ab(o, 2080), 4); o += 2304
        sgb = [af(o, 512), af(o + 512, 512)]; o += 1024
        junk = ab(o, 1024); o += 1024
        hblk5b = ab(o, 1024); o += 1024
        assert o <= X_OT + 2304
        XK = ['xacc%d' % t for t in range(9)]
        for t in range(8):
            S.dma('sp', xacc[:, t, :], xext[1024 + 128 * t:1024 + 128 * (t + 1), :], writes=[XK[t]])
        S.dma('sp', xacc[0:16, 8, :], xs_d, writes=[XK[8]])
        TNP = [128] * 8 + [16]
        load_gbc(1)
        it = 0
        hfT = mixT
        hb5 = [hblk5, hblk5b]

        def a5_norm(t):
            norm_tile(xacc[:, t, :], XK[t], hb5[t % 2], 'hblk5_%d' % (t % 2), np_=TNP[t])

        def a5_T(t):
            transposes(hb5[t % 2], 'hblk5_%d' % (t % 2), lambda c0: hfT[:, c0:c0 + 8, 128 * t:128 * t + TNP[t]], 'mixT%d' % t, banks=(4, 5), np_=TNP[t])

        for jh in range(8):
            if jh >= 4:
                a5_load(jh)
            ws, wkey = a5_buf(jh)
            for t in range(9):
                bk = it % 4
                it += 1
                np_ = TNP[t]
                for k in range(KC):
                    S.op('pe', lambda e: e.matmul(PS[bk][0:np_, 0:256], lhsT=mixT[:, k, 128 * t:128 * t + np_], rhs=ws[:, k, :], start=(k == 0), stop=(k == KC - 1)),
                         ['mixT%d' % t, wkey], [PSK[bk]])
                xs_ = xacc[0:np_, t, 256 * jh:256 * (jh + 1)]
                S.op('dve', lambda e: e.tensor_tensor(out=xs_, in0=PS[bk][0:np_, 0:256], in1=xs_, op=ALU.add), [PSK[bk], XK[t]], [XK[t]])
        a5_norm(0)
        for t in range(9):
            if t + 1 < 9:
                a5_norm(t + 1)
            a5_T(t)
        dbg('x1', xacc[:, :, :], list(XK))
        dbg('hfT', hfT[:, :, :], list(MK_ALL))
        if stop_after == 'A5':
            S.barrier(('sp',))
            S.emit()
            return nc, dbg_outs

        def final_norm(t):
            np_ = TNP[t]
            xt_ = xacc[0:np_, t, :]
            S.op('act', lambda e: e.activation(out=junk[0:np_, :], in_=xt_, func=AF.Square, accum_out=ssq[0:np_, :]), [XK[t]], ['junk', 'ssq'])
            S.op('act', lambda e: e.activation(out=rs[0:np_, :], in_=ssq[0:np_, :], func=AF.Ln, scale=1.0 / D, bias=eps_norm[0:np_, :]), ['ssq', 'epsc'], ['rs'])
            S.op('act', lambda e: e.activation(out=rstd[0:np_, :], in_=rs[0:np_, :], func=AF.Exp, scale=-0.5), ['rs'], ['rstd'])
            S.op('dve', lambda e: e.scalar_tensor_tensor(out=xt_, in0=xt_, scalar=rstd[0:np_, :], in1=gbc[0:np_, :], op0=ALU.mult, op1=ALU.mult), [XK[t], 'rstd', 'gbc'], [XK[t]])
            if t < 8:
                S.dma('sp', y_d[128 * t:128 * (t + 1), :], xt_, reads=[XK[t]], writes=['y'])
            else:
                S.dma('sp', ys_d, xt_, reads=[XK[t]], writes=['ys'])

        load_gbc(2)
        it = 0
        itd = 0
        for hs in range(NHS):
            wg = wload(0, w_gate[:, 512 * hs:512 * (hs + 1)], 16, 512)
            wu = wload(1, w_up[:, 512 * hs:512 * (hs + 1)], 16, 512)
            wd = W[2][:, 0:8192].rearrange("p (k n) -> p k n", k=4)
            S.dma('pool', wd, w_down[512 * hs:512 * (hs + 1), :].rearrange("(k p) n -> p k n", p=128), writes=['W2', 'W2b'])
            for n4 in range(4):
                wsl = slice(n4 * 128, (n4 + 1) * 128)
                for (c0, nc_) in TGN:
                    q = it % 2
                    it += 1
                    pg, pu = q, 2 + q
                    for k in range(KC):
                        S.op('pe', lambda e, k=k, pg=pg, c0=c0, nc_=nc_, wsl=wsl: e.matmul(PS[pg][:, 0:nc_], lhsT=wg[:, k, wsl], rhs=hfT[:, k, c0:c0 + nc_], start=(k == 0), stop=(k == KC - 1)),
                             mxk(c0, nc_) + ['W0'], [PSK[pg]])
                    for k in range(KC):
                        S.op('pe', lambda e, k=k, pu=pu, c0=c0, nc_=nc_, wsl=wsl: e.matmul(PS[pu][:, 0:nc_], lhsT=wu[:, k, wsl], rhs=hfT[:, k, c0:c0 + nc_], start=(k == 0), stop=(k == KC - 1)),
                             mxk(c0, nc_) + ['W1'], [PSK[pu]])
                    S.op('act', lambda e, q=q, pg=pg, nc_=nc_: e.activation(out=sgb[q][:, 0:nc_], in_=PS[pg][:, 0:nc_], func=AF.Silu), [PSK[pg]], ['sgb%d' % q])
                    S.op('dve', lambda e, q=q, pu=pu, n4=n4, c0=c0, nc_=nc_: e.tensor_tensor(out=aq[:, n4, c0:c0 + nc_], in0=PS[pu][:, 0:nc_], in1=sgb[q][:, 0:nc_], op=ALU.mult),
                         [PSK[pu], 'sgb%d' % q], ['aq'])
            for t in range(9):
                for j in range(4):
                    bk = 4 + (itd % 4)
                    itd += 1
                    np_ = TNP[t]
                    for k in range(4):
                        S.op('pe', lambda e: e.matmul(PS[bk][0:np_, :], lhsT=aq[:, k, 128 * t:128 * t + np_], rhs=wd[:, k, 512 * j:512 * (j + 1)], start=(k == 0), stop=(k == 3)),
                             ['aq', 'W2', 'W2b'], [PSK[bk]])
                    xs_ = xacc[0:np_, t, 512 * j:512 * (j + 1)]
                    S.op('dve', lambda e: e.tensor_tensor(out=xs_, in0=PS[bk][0:np_, :], in1=xs_, op=ALU.add), [PSK[bk], XK[t]], [XK[t]])
                if hs == NHS - 1:
                    final_norm(t)
        S.barrier(('sp',))
        S.emit()
        return nc, dbg_outs


def _rope_tab(pos):
    inv = np.power(np.float32(500000.0), -np.arange(0, 32, 2, dtype=np.float32) / np.float32(32)).astype(np.float32)
    ang = pos.astype(np.float32)[:, None] * inv[None, :]
    c = np.cos(ang).astype(np.float32)
    s = np.sin(ang).astype(np.float32)
    return np.concatenate([c, c, -s, s], axis=1).astype(np.float32)


def _consts(half):
    k = np.arange(128)[:, None]
    q = np.arange(128)[None, :]
    own = np.where(k <= q, 0.0, NEG)
    prev = np.where(k >= q, 0.0, NEG)
    prevh = prev if half == 1 else np.full((128, 128), NEG)
    g2 = own.copy()
    if half == 0:
        g2[:64, :] = NEG
    masks = np.stack([own, prev, prevh, g2], axis=1).astype(np.float32)
    p = np.arange(128)[:, None]
    ih = np.arange(16)[None, :]
    i_of = ih // 4
    m0 = (p >= i_of).astype(np.float32)
    j = p % 32
    valid = (j < 4)
    mn0 = (valid & (j <= i_of)).astype(np.float32)
    mn12 = (valid & (j == i_of)).astype(np.float32)
    smask = np.stack([m0, mn0, mn12], axis=1).astype(np.float32)
    sel = np.zeros((128, 4, 128), np.float32)
    m = np.arange(128)
    for i in range(4):
        sel[32 * (m // 32) + i, i, m] = 1.0
    ind = np.zeros((128, 4), np.float32)
    for s in range(4):
        ind[32 * s:32 * s + 32, s] = 1.0
    hv = np.full((128, 1), float(half), np.float32)
    return masks, smask, sel, ind, hv


def _tabs(half):
    T0 = 1024 * half
    out = np.zeros((3, 128, 16, 64), np.float32)
    p = np.arange(128)
    for g in range(3):
        d = GROUPS[g][1]
        for bi, (r, b, kind) in enumerate(group_blocks(g)):
            e = r + d * (128 * b + p)
            pos = np.maximum(T0 - 1024 + e, 0)
            out[g, :, bi, :] = _rope_tab(pos)
    posS = PAST + np.minimum(p % 32, 3)
    tabS = _rope_tab(posS)
    return out, tabS


def make_core_inputs(c, inp):
    b, half = c // 2, c % 2
    xp = inp['x_prompt'][b]
    if half == 1:
        xext = xp
    else:
        xext = np.concatenate([np.zeros((1024, D), np.float32), xp[:1024]], axis=0)
    sl = slice(4 * c, 4 * c + 4)
    m = {}
    m['xext'] = np.ascontiguousarray(xext, dtype=np.float32)
    m['xs'] = np.ascontiguousarray(inp['x_sample'][sl].reshape(16, D))
    m['state'] = np.ascontiguousarray(inp['state_conv'][0, sl].reshape(120, 1024))
    caches = ((inp['cache_k_w128'], inp['cache_v_w128']), (inp['cache_k_w512'], inp['cache_v_w512']),
              (inp['cache_k_w2048'], inp['cache_v_w2048']))
    for g, (wb, d) in enumerate(GROUPS):
        m['ck%d' % g] = np.ascontiguousarray(caches[g][0][0, sl].reshape(4, wb, 512))
        m['cv%d' % g] = np.ascontiguousarray(caches[g][1][0, sl].reshape(4, wb, 512))
    m['w_in'] = inp['w_in'][0]
    m['w_pw'] = inp['w_pw'][0]
    m['w_o'] = inp['w_o_att'][0]
    m['w_out'] = inp['w_out'][0]
    m['w_gate'] = inp['w_gate'][0]
    m['w_up'] = inp['w_up'][0]
    m['w_down'] = inp['w_down'][0]
    vecs = np.zeros((384, 128), np.float32)
    vecs[0:16] = inp['g_mix'][0].reshape(16, 128)
    vecs[16:32] = inp['b_glu'][0].reshape(16, 128)
    vecs[32:40] = inp['b_dw'][0].reshape(8, 128)
    vecs[40:48] = inp['ln_g'][0].reshape(8, 128)
    vecs[48:56] = inp['ln_b'][0].reshape(8, 128)
    vecs[56:72] = inp['b_pw'][0].reshape(16, 128)
    vecs[128:128 + 248] = inp['w_dw'][0].reshape(31 * 8, 128)
    m['vecs'] = vecs
    m['gb'] = np.stack([inp['g_mix'][0], inp['g_ffn'][0], inp['g_final']], axis=0).astype(np.float32)
    tabs, tabS = _tabs(half)
    m['tabs'] = tabs
    m['tabS'] = tabS
    masks, smask, sel, ind, hv = _consts(half)
    m['masks'] = masks
    m['ident'] = np.eye(128, dtype=np.float32)
    m['smask'] = smask
    m['sel'] = sel
    m['ind'] = ind
    m['hv'] = hv
    return m


def assemble(results):
    f32 = np.float32
    y_prompt = np.zeros((4, 2048, D), f32)
    y_sample = np.zeros((32, 4, D), f32)
    conv_p = np.zeros((1, 4, 30, 1024), f32)
    conv_s = np.zeros((1, 32, 30, 1024), f32)
    kps = [np.zeros((1, 4, min(wb, 2048), 4, 128), f32) for wb, _ in GROUPS]
    vps = [np.zeros((1, 4, min(wb, 2048), 4, 128), f32) for wb, _ in GROUPS]
    kss = [np.zeros((1, 32, wb, 4, 128), f32) for wb, _ in GROUPS]
    vss = [np.zeros((1, 32, wb, 4, 128), f32) for wb, _ in GROUPS]
    for c, r in enumerate(results):
        b, half = c // 2, c % 2
        y_prompt[b, 1024 * half:1024 * (half + 1)] = r['y']
        y_sample[4 * c:4 * c + 4] = r['ys'].reshape(4, 4, D)
        conv_s[0, 4 * c:4 * c + 4] = r['convs']
        for g, (wb, d) in enumerate(GROUPS):
            kss[g][0, 4 * c:4 * c + 4] = r['kso%d' % g].reshape(4, wb, 4, 128)
            vss[g][0, 4 * c:4 * c + 4] = r['vso%d' % g].reshape(4, wb, 4, 128)
        kps[2][0, b, 1024 * half:1024 * (half + 1)] = r['kp2'].reshape(1024, 4, 128)
        vps[2][0, b, 1024 * half:1024 * (half + 1)] = r['vp2'].reshape(1024, 4, 128)
        if half == 1:
            conv_p[0, b] = r['convp']
            kps[0][0, b] = r['kp0'].reshape(128, 4, 128)
            vps[0][0, b] = r['vp0'].reshape(128, 4, 128)
            kps[1][0, b] = r['kp1'].reshape(512, 4, 128)
            vps[1][0, b] = r['vp1'].reshape(512, 4, 128)
    return (y_prompt, y_sample, conv_p, conv_s,
            kps[0], kss[0], vps[0], vss[0],
            kps[1], kss[1], vps[1], vss[1],
            kps[2], kss[2], vps[2], vss[2])


_PROG = {}


def kernel(**inputs):
    inp = {k: np.asarray(v) for k, v in inputs.items()}
    if 'nc' not in _PROG:
        _PROG['nc'] = build_program()[0]
    nc = _PROG['nc']
    in_maps = [make_core_inputs(c, inp) for c in range(8)]
    res = run_bass_kernel_spmd(nc, in_maps, core_ids=list(range(8)))
    return assemble(res.results)
```
